# Optimizing a Trainium2 kernel written in Bass

```python
import math
import jax
import jax.numpy as jnp
from jax import lax
import numpy as np

D_MODEL = 1024
BATCH = 2
SEQ = 8192
DEPTH = 4
DEC_BATCH = 32
DEC_SEQ = 2048
PAST_LEN = 128

EPS = 1e-6
H_A = 8
DK_A = 128
DV_A = 128
CONV_A = 5
CHUNK_A = 64
DECAY_PROJ_SCALE = 0.1
DIL_PAIRS = ((128, 1), (512, 4), (2048, 16))
H_B_GROUP = 4
H_B = H_B_GROUP * len(DIL_PAIRS)
DH_B = 64
BLOCK_B = 64
N_BUCKETS = 32
REL_MAX_DIST = 1024
D_FF = 2816
CONV_FF = 3

W_A = H_A * DK_A
W_A_V = H_A * DV_A
W_B = H_B * DH_B
W_B_OUT = H_B_GROUP * DH_B
OFF_QKV_A = 0
OFF_Z_A = OFF_QKV_A + 2 * W_A + W_A_V
OFF_BETA_A = OFF_Z_A + W_A_V
OFF_ALPHA_A = OFF_BETA_A + 2 * H_A
OFF_QKV_B = OFF_ALPHA_A + 2 * H_A
OFF_GATE = OFF_QKV_B + 3 * W_B
N_IN = OFF_GATE + 2 * D_MODEL

kernel_name = 'hybrid_gdn_dilated_attn_encoder'


def rms_norm(x, g):
    xf = x.astype(jnp.float32)
    y = xf * lax.rsqrt(jnp.mean(xf * xf, axis=-1, keepdims=True) + EPS)
    return (y * g.astype(jnp.float32)).astype(x.dtype)


def l2_norm(x):
    return x * lax.rsqrt(jnp.sum(x * x, axis=-1, keepdims=True) + EPS)


def depthwise_conv_centred(x, w):
    pad = w.shape[0] // 2
    return lax.conv_general_dilated(
        x, w[:, None, :].astype(x.dtype), window_strides=(1,), padding=((pad, pad),),
        dimension_numbers=('NWC', 'WIO', 'NWC'), feature_group_count=x.shape[-1])


def gated_delta_rule_chunked(q, k, v, g, beta):
    bsz, t, h, dk = q.shape
    dv = v.shape[-1]
    n = t // CHUNK_A

    def chunks(a):
        return jnp.moveaxis(a.reshape(bsz, n, CHUNK_A, h, *a.shape[3:]), 3, 2)

    qc, kc, vc, bc = chunks(q), chunks(k), chunks(v), chunks(beta)
    gc = jnp.cumsum(chunks(g), axis=-1)
    causal = jnp.tril(jnp.ones((CHUNK_A, CHUNK_A), bool))
    strict = jnp.tril(jnp.ones((CHUNK_A, CHUNK_A), bool), -1)
    gdiff = gc[..., :, None] - gc[..., None, :]
    decay = jnp.where(causal, jnp.exp(jnp.where(causal, gdiff, 0.0)), 0.0)
    kk = jnp.einsum('bnhid,bnhjd->bnhij', kc, kc)
    a_mat = jnp.where(strict, bc[..., :, None] * kk * decay, 0.0) + jnp.eye(CHUNK_A, dtype=jnp.float32)
    rhs = jnp.concatenate([vc * bc[..., None], kc * (bc * jnp.exp(gc))[..., None]], axis=-1)
    sol = lax.linalg.triangular_solve(a_mat, rhs, left_side=True, lower=True, unit_diagonal=True)
    u, w = sol[..., :dv], sol[..., dv:]
    qk = jnp.where(causal, jnp.einsum('bnhid,bnhjd->bnhij', qc, kc) * decay, 0.0)
    q_dec = qc * jnp.exp(gc)[..., None]
    k_dec = kc * jnp.exp(gc[..., -1:] - gc)[..., None]
    g_last = jnp.exp(gc[..., -1])

    def step(s, xs):
        u_i, w_i, qk_i, qd_i, kd_i, gl_i = xs
        v_new = u_i - jnp.einsum('bhcd,bhde->bhce', w_i, s)
        o_i = jnp.einsum('bhcd,bhde->bhce', qd_i, s) + jnp.einsum('bhij,bhje->bhie', qk_i, v_new)
        s = s * gl_i[..., None, None] + jnp.einsum('bhcd,bhce->bhde', kd_i, v_new)
        return s, o_i

    xs = tuple(jnp.moveaxis(a, 1, 0) for a in (u, w, qk, q_dec, k_dec, g_last))
    s0 = jnp.zeros((bsz, h, dk, dv), jnp.float32)
    _, o = lax.scan(step, s0, xs)
    o = jnp.moveaxis(jnp.moveaxis(o, 0, 1), 3, 2)
    return o.reshape(bsz, t, h, dv)


def gated_deltanet(qkv, z, beta_raw, alpha_raw, conv_w, a_log, dt_bias, norm_g):
    bsz, t = qkv.shape[:2]
    f32 = jnp.float32
    qkv = jax.nn.silu(depthwise_conv_centred(qkv, conv_w)).astype(f32)
    q = l2_norm(qkv[..., :W_A].reshape(bsz, t, H_A, DK_A)) * (DK_A ** -0.5)
    k = l2_norm(qkv[..., W_A:2 * W_A].reshape(bsz, t, H_A, DK_A))
    v = qkv[..., 2 * W_A:].reshape(bsz, t, H_A, DV_A)
    beta = jax.nn.sigmoid(beta_raw.astype(f32))
    g = -jnp.exp(a_log.astype(f32)) * jax.nn.softplus(alpha_raw.astype(f32) + dt_bias.astype(f32))
    o_fwd = gated_delta_rule_chunked(q, k, v, g[:, :, 0], beta[:, :, 0])
    rev = lambda a: jnp.flip(a, axis=1)
    o_bwd = rev(gated_delta_rule_chunked(rev(q), rev(k), rev(v), rev(g[:, :, 1]), rev(beta[:, :, 1])))
    o = rms_norm(o_fwd + o_bwd, norm_g) * jax.nn.silu(z.astype(f32).reshape(bsz, t, H_A, DV_A))
    return o.reshape(bsz, t, W_A_V).astype(z.dtype)


def t5_bucket(rel):
    nb = N_BUCKETS // 2
    max_exact = nb // 2
    n = jnp.abs(rel)
    large = max_exact + (jnp.log(jnp.maximum(n, 1).astype(jnp.float32) / max_exact)
                         / math.log(REL_MAX_DIST / max_exact) * (nb - max_exact)).astype(jnp.int32)
    large = jnp.minimum(large, nb - 1)
    return jnp.where(rel > 0, nb, 0) + jnp.where(n < max_exact, n, large)


def dilated_group_attention(q, k, v, bias_table, dil, radius):
    bsz, t, hg, dh = q.shape
    L = t // dil
    nb = -(-L // BLOCK_B)
    lp = nb * BLOCK_B

    def residues(a):
        a = a.reshape(bsz, L, dil, hg, dh).swapaxes(1, 2)
        return jnp.pad(a, ((0, 0), (0, 0), (0, lp - L), (0, 0), (0, 0)))

    def kv_blocks(a):
        a = jnp.pad(residues(a), ((0, 0), (0, 0), (BLOCK_B, BLOCK_B), (0, 0), (0, 0)))
        a = a.reshape(bsz, dil, nb + 2, BLOCK_B, hg, dh)
        return jnp.concatenate([a[:, :, :-2], a[:, :, 1:-1], a[:, :, 2:]], axis=3)

    qb = residues(q).reshape(bsz, dil, nb, BLOCK_B, hg, dh)
    kb, vb = kv_blocks(k), kv_blocks(v)
    qi = jnp.arange(BLOCK_B)[:, None]
    kj = jnp.arange(3 * BLOCK_B)[None, :] - BLOCK_B
    rel = kj - qi
    bias = jnp.take(bias_table, t5_bucket(rel * dil), axis=0)
    bias = jnp.moveaxis(bias, -1, 0).astype(jnp.float32)
    kpos = jnp.arange(nb)[:, None, None] * BLOCK_B + kj[None]
    valid = (jnp.abs(rel)[None] <= radius) & (kpos >= 0) & (kpos < L)
    s = jnp.einsum('brnqhd,brnkhd->brnhqk', qb, kb).astype(jnp.float32) * (dh ** -0.5) + bias
    s = jnp.where(valid[None, None, :, None], s, -jnp.inf)
    m = jnp.max(s, axis=-1, keepdims=True)
    p = jnp.exp(s - m)
    den = jnp.sum(p, axis=-1)
    o = jnp.einsum('brnhqk,brnkhd->brnqhd', p, vb.astype(jnp.float32)) / jnp.moveaxis(den, -1, -2)[..., None]
    lse = m[..., 0] + jnp.log(den)
    o = o.reshape(bsz, dil, lp, hg, dh)[:, :, :L].swapaxes(1, 2).reshape(bsz, t, hg, dh)
    lse = lse.swapaxes(3, 4).reshape(bsz, dil, lp, hg)[:, :, :L].swapaxes(1, 2).reshape(bsz, t, hg)
    return o, lse


def dilated_attention(q, k, v, qn_g, kn_g, rel_bias):
    bsz, t = q.shape[:2]
    q, k = rms_norm(q, qn_g), rms_norm(k, kn_g)
    outs, lses = [], []
    for gi, (window, dil) in enumerate(DIL_PAIRS):
        sl = slice(gi * H_B_GROUP, (gi + 1) * H_B_GROUP)
        o, lse = dilated_group_attention(q[:, :, sl], k[:, :, sl], v[:, :, sl], rel_bias[:, sl], dil, window // (2 * dil))
        outs.append(o)
        lses.append(lse)
    wts = jax.nn.softmax(jnp.stack(lses, axis=2), axis=2)
    o = jnp.einsum('btghd,btgh->bthd', jnp.stack(outs, axis=2), wts)
    return o.reshape(bsz, t, W_B_OUT).astype(v.dtype)


def encoder_layer(x, ln1_g, w_in, conv_a, a_log, dt_bias, norm_a, qn_b, kn_b, rel_bias,
                  w_oa, w_ob, w_out, ln2_g, w_up, w_gate, conv_ff, conv_ff_b, w_down):
    bsz, t, _ = x.shape
    h = rms_norm(x, ln1_g)
    p = h @ w_in
    o_a = gated_deltanet(
        p[..., OFF_QKV_A:OFF_Z_A], p[..., OFF_Z_A:OFF_BETA_A],
        p[..., OFF_BETA_A:OFF_ALPHA_A].reshape(bsz, t, 2, H_A),
        p[..., OFF_ALPHA_A:OFF_QKV_B].reshape(bsz, t, 2, H_A),
        conv_a, a_log, dt_bias, norm_a)
    qkv_b = p[..., OFF_QKV_B:OFF_GATE].reshape(bsz, t, 3, H_B, DH_B)
    o_b = dilated_attention(qkv_b[:, :, 0], qkv_b[:, :, 1], qkv_b[:, :, 2], qn_b, kn_b, rel_bias)
    gates = jax.nn.sigmoid(p[..., OFF_GATE:].astype(jnp.float32)).reshape(bsz, t, 2, D_MODEL)
    mixed = gates[:, :, 0] * (o_a @ w_oa) + gates[:, :, 1] * (o_b @ w_ob)
    x = x + (mixed.astype(x.dtype) @ w_out)
    h = rms_norm(x, ln2_g)
    u = jax.nn.silu(depthwise_conv_centred(h @ w_up, conv_ff) + conv_ff_b)
    return x + (u * (h @ w_gate)) @ w_down


def setup_inputs(seed: int = 0) -> dict:
    key = jax.random.key(seed)
    ks = jax.random.split(key, 24)
    f32 = jnp.float32
    nrm = lambda kk, shape, scale: jax.random.normal(kk, shape, f32) * scale
    x_prompt = nrm(ks[0], (BATCH, SEQ, D_MODEL), 1.0)
    x_sample = nrm(ks[1], (DEC_BATCH, DEC_SEQ, D_MODEL), 1.0)
    ln1_g = 1.0 + nrm(ks[2], (DEPTH, D_MODEL), 0.02)
    w_in = nrm(ks[3], (DEPTH, D_MODEL, N_IN), D_MODEL ** -0.5)
    w_in = w_in.at[:, :, OFF_BETA_A:OFF_QKV_B].multiply(DECAY_PROJ_SCALE)
    conv_a = nrm(ks[4], (DEPTH, CONV_A, 2 * W_A + W_A_V), CONV_A ** -0.5)
    a_log = jnp.log(jax.random.uniform(ks[5], (DEPTH, 2, H_A), f32, minval=1.0, maxval=16.0))
    dt = jnp.exp(jax.random.uniform(ks[6], (DEPTH, 2, H_A), f32, minval=math.log(1e-3), maxval=math.log(1e-1)))
    dt_bias = dt + jnp.log(-jnp.expm1(-dt))
    norm_a = 1.0 + nrm(ks[7], (DEPTH, DV_A), 0.02)
    qn_b = 1.0 + nrm(ks[8], (DEPTH, DH_B), 0.02)
    kn_b = 1.0 + nrm(ks[9], (DEPTH, DH_B), 0.02)
    rel_bias = nrm(ks[10], (N_BUCKETS, H_B), 0.5)
    w_oa = nrm(ks[11], (DEPTH, W_A_V, D_MODEL), W_A_V ** -0.5)
    w_ob = nrm(ks[12], (DEPTH, W_B_OUT, D_MODEL), W_B_OUT ** -0.5)
    w_out = nrm(ks[13], (DEPTH, D_MODEL, D_MODEL), D_MODEL ** -0.5)
    ln2_g = 1.0 + nrm(ks[14], (DEPTH, D_MODEL), 0.02)
    w_up = nrm(ks[15], (DEPTH, D_MODEL, D_FF), D_MODEL ** -0.5)
    w_gate = nrm(ks[16], (DEPTH, D_MODEL, D_FF), D_MODEL ** -0.5)
    conv_ff = nrm(ks[17], (DEPTH, CONV_FF, D_FF), CONV_FF ** -0.5)
    conv_ff_b = nrm(ks[18], (DEPTH, D_FF), 0.02)
    w_down = nrm(ks[19], (DEPTH, D_FF, D_MODEL), D_FF ** -0.5)
    return {'x_prompt': x_prompt, 'x_sample': x_sample, 'ln1_g': ln1_g, 'w_in': w_in, 'conv_a': conv_a,
            'a_log': a_log, 'dt_bias': dt_bias, 'norm_a': norm_a, 'qn_b': qn_b, 'kn_b': kn_b,
            'rel_bias': rel_bias, 'w_oa': w_oa, 'w_ob': w_ob, 'w_out': w_out, 'ln2_g': ln2_g,
            'w_up': w_up, 'w_gate': w_gate, 'conv_ff': conv_ff, 'conv_ff_b': conv_ff_b, 'w_down': w_down}


def reference(x_prompt, x_sample, ln1_g, w_in, conv_a, a_log, dt_bias, norm_a, qn_b, kn_b, rel_bias,
              w_oa, w_ob, w_out, ln2_g, w_up, w_gate, conv_ff, conv_ff_b, w_down):
    def trunk(x):
        for l in range(DEPTH):
            x = encoder_layer(x, ln1_g[l], w_in[l], conv_a[l], a_log[l], dt_bias[l], norm_a[l], qn_b[l], kn_b[l],
                              rel_bias, w_oa[l], w_ob[l], w_out[l], ln2_g[l], w_up[l], w_gate[l], conv_ff[l],
                              conv_ff_b[l], w_down[l])
        return x
    y_prompt = trunk(x_prompt)
    y_sample = trunk(x_sample)
    return (y_prompt, y_sample)
```

```python
import numpy as np
from contextlib import ExitStack
import concourse.bass as bass
import concourse.mybir as mybir
from concourse.bass_utils import run_bass_kernel_spmd

F32 = mybir.dt.float32
BF16 = mybir.dt.bfloat16
import os as _os
F32R = mybir.dt.float32r if _os.environ.get('KF_FP32R', '1') == '1' else mybir.dt.float32
KF_PEND = _os.environ.get('KF_PEND', '1') == '1'
KF_LOOK = int(_os.environ.get('KF_LOOK', '3'))
KF_STQ = _os.environ.get('KF_STQ', 'act')
ALU = mybir.AluOpType
AF = mybir.ActivationFunctionType

D = 1024
NIN = 8480
DFF = 2816
NFF = 22
OFF_Z, OFF_BETA, OFF_ALPHA, OFF_QB = 3072, 4096, 4112, 4128
OFF_KB, OFF_VB, OFF_GATE = 4128 + 768, 4128 + 1536, 6432
EPS = 1e-6
BIG = 30000.0
NF = 2432
FOFF = 1216
DILS = (1, 4, 16)
OFFS = (1, 2, 8)


class Res:
    __slots__ = ("w", "r")

    def __init__(self):
        self.w = {}
        self.r = {}


class PRes(Res):
    __slots__ = ()


class MK:
    def __init__(self, nc):
        self.nc = nc
        self.es = ExitStack()
        self.engs = {"pe": nc.tensor, "act": nc.scalar, "dve": nc.vector, "pool": nc.gpsimd, "sp": nc.sync}
        self.sem = {}
        for k in ["pe", "act", "dve", "pool"]:
            self.sem[k] = self.es.enter_context(nc.semaphore("s_" + k))
        self.cnt = {k: 0 for k in ["pe", "act", "dve", "pool"]}
        self.rings = {}
        for pre, n in (("d", 32), ("e", 24), ("c", 32)):
            for i in range(n):
                self.sem["%s%d" % (pre, i)] = self.es.enter_context(nc.semaphore("%s%d" % (pre, i)))
            self.rings[pre] = {"n": n, "cnt": [0] * n, "next": 0}
        self.seen = {e: {} for e in self.engs}
        self.ninstr = 0
        self.stack = [self.es]
        self.uid = 0

    def push(self):
        es = ExitStack()
        self.stack.append(es)

    def pop(self):
        self.barrier()
        self.stack.pop().close()

    def sb(self, shape, dt, name=None):
        self.uid += 1
        return self.stack[-1].enter_context(self.nc.sbuf_tensor("t%d" % self.uid, list(shape), dt))

    def ps(self, shape, dt=F32):
        self.uid += 1
        return self.stack[-1].enter_context(self.nc.psum_tensor("p%d" % self.uid, list(shape), dt))

    def need(self, eng, tok):
        key, val = tok
        if val <= 0:
            return
        if key == "pe" and eng == "pe":
            return
        s = self.seen[eng]
        if s.get(key, 0) >= val:
            return
        self.engs[eng].wait_ge(self.sem[key], val)
        self.ninstr += 1
        s[key] = val

    def _deps(self, eng, reads, writes):
        for r in reads:
            if isinstance(r, PRes):
                for t in r.w.items():
                    if t[0] != eng:
                        self.need(eng, t)
                continue
            for t in r.w.items():
                self.need(eng, t)
        for w in writes:
            if isinstance(w, PRes):
                for t in w.w.items():
                    if t[0] != eng:
                        self.need(eng, t)
                continue
            for t in w.w.items():
                self.need(eng, t)
            for t in w.r.items():
                self.need(eng, t)

    def _commit(self, key, val, reads, writes, acc):
        for r in reads:
            if isinstance(r, PRes):
                r.w[key] = val
            else:
                r.r[key] = val
        for w in writes:
            if isinstance(w, PRes):
                w.w[key] = val
            elif acc:
                w.w[key] = val
            else:
                w.w = {key: val}
                w.r = {}

    def op(self, eng, fn, reads=(), writes=(), acc=False):
        self._deps(eng, reads, writes)
        ins = fn(self.engs[eng])
        self.cnt[eng] += 1
        ins.then_inc(self.sem[eng], 1)
        self.ninstr += 1
        self._commit(eng, self.cnt[eng], reads, writes, acc)

    def dma(self, out, in_, reads=(), writes=(), q="sp", acc=False, cast=False, **kw):
        if q == "pool" and not cast:
            q = KF_STQ
        pre = "c" if cast else ("d" if q in ("sp", "act") else "e")
        assert (q == "pool") == (pre in ("c", "e"))
        rg = self.rings[pre]
        k = rg["next"]
        rg["next"] = (k + 1) % rg["n"]
        key = "%s%d" % (pre, k)
        cnts = rg["cnt"]
        self.need(q, (key, cnts[k]))
        self._deps(q, reads, writes)
        ins = self.engs[q].dma_start(out=out, in_=in_, **kw)
        cnts[k] += 16
        ins.then_inc(self.sem[key], 16)
        self.ninstr += 1
        self._commit(key, cnts[k], reads, writes, acc)

    def barrier(self):
        for eng in ["sp", "pe", "act", "dve", "pool"]:
            for k in ["pe", "act", "dve", "pool"]:
                if k != eng:
                    self.need(eng, (k, self.cnt[k]))
            for pre in ("d", "e"):
                rg = self.rings[pre]
                for i in range(rg["n"]):
                    self.need(eng, ("%s%d" % (pre, i), rg["cnt"][i]))

    def close(self):
        self.es.close()


class Ring:
    def __init__(self, items):
        self.items = items
        self.i = 0

    def get(self):
        it = self.items[self.i]
        self.i = (self.i + 1) % len(self.items)
        return it


def pieces(total, step=512):
    out = []
    c = 0
    while c < total:
        out.append((c, min(step, total - c)))
        c += step
    return out


def build(NSEG, SEG, DEPTH, FB=1024):
    NT = NSEG * SEG
    TPS = SEG // 128
    NTILE = NT // 128
    nc = bass.Bass("TRN2", target_bir_lowering=False)
    dt_in = lambda n, s: nc.dram_tensor(n, list(s), F32, kind="ExternalInput").ap()
    xT = dt_in("xT", [D, NT + 4])
    flags_d = dt_in("flags", [128, NSEG + 1])
    cst_d = dt_in("cst", [128, 9 * 128])
    oh_d = dt_in("oh", [32, NF])
    valid_d = dt_in("valid", [12, NF])
    W = {}
    for n, s in [("ln1_g", [DEPTH, D]), ("w_in", [DEPTH, D, NIN]), ("conv_a", [DEPTH, 5, 3072]),
                 ("a_log", [DEPTH, 16]), ("dt_bias", [DEPTH, 16]), ("norm_a", [DEPTH, 128]),
                 ("qn_b", [DEPTH, 64]), ("kn_b", [DEPTH, 64]), ("rel_bias", [32, 12]),
                 ("w_oa", [DEPTH, D, D]), ("w_ob", [DEPTH, 256, D]), ("w_out", [DEPTH, D, D]),
                 ("ln2_g", [DEPTH, D]), ("w_up", [DEPTH, D, DFF]), ("w_gate", [DEPTH, D, DFF]),
                 ("conv_ff", [DEPTH, 3, DFF]), ("conv_ff_b", [DEPTH, DFF]), ("w_down", [DEPTH, DFF, D])]:
        W[n] = dt_in(n, s)
    yT = nc.dram_tensor("yT", [D, NT], F32, kind="ExternalOutput").ap()

    def scr(n, s, dt=BF16):
        return nc.dram_tensor(n, list(s), dt).ap()

    WB = {n: scr("b_" + n, W[n].shape) for n in ["w_in", "w_oa", "w_ob", "w_out", "w_up", "w_gate", "w_down"]}
    XA = scr("XA", [D, NT + 4], F32)
    XB = scr("XB", [D, NT + 4], F32)
    cq = scr("cq", [8, 128, NT])
    ck = scr("ck", [8, 128, NT])
    kt = scr("kt", [NT, D])
    vt = scr("vt", [NT, D])
    zt = scr("zt", [D, NT])
    gbd = scr("gbd", [NT, 32], F32)
    aq = scr("aq", [768, NT])
    ak = scr("ak", [768, NT])
    av = scr("av", [NT, 768])
    gt = scr("gt", [2048, NT])
    ofd = scr("ofd", [NT, D], F32)
    oag = scr("oag", [D, NT])
    obd = scr("obd", [256, NT])
    Fd = scr("Fd", [12, NF], F32)

    mk = MK(nc)
    op, dma = mk.op, mk.dma

    cst = mk.sb([128, 9 * 128], F32)
    Rc = Res()
    dma(cst[:], cst_d, writes=[Rc])
    IDF, UF, LF = cst[:, 0:128], cst[:, 128:256], cst[:, 256:384]
    PM_L_S, PM_L_I, PM_U_S, PM_U_I = (cst[:, 384:512], cst[:, 512:640], cst[:, 640:768], cst[:, 768:896])
    ONESF = cst[:, 896:1024]
    cb = mk.sb([128, 3 * 128], BF16)
    op("dve", lambda e: e.tensor_copy(out=cb[:, 0:128], in_=cst[:, 0:128]), reads=[Rc], writes=[Rc])
    op("dve", lambda e: e.tensor_copy(out=cb[:, 128:256], in_=cst[:, 896:1024]), reads=[Rc], writes=[Rc])
    op("dve", lambda e: e.tensor_copy(out=cb[:, 256:384], in_=cst[:, 1024:1152]), reads=[Rc], writes=[Rc])
    IDB, ONESB, BD64B = cb[:, 0:128], cb[:, 128:256], cb[:, 256:384]
    cr = mk.sb([128, 384], F32R)
    op("dve", lambda e: e.tensor_copy(out=cr[:, 256:384], in_=cst[:, 0:128]), reads=[Rc], writes=[Rc])
    op("dve", lambda e: e.tensor_copy(out=cr[:, 0:128], in_=cst[:, 128:256]), reads=[Rc], writes=[Rc])
    op("dve", lambda e: e.tensor_copy(out=cr[:, 128:256], in_=cst[:, 256:384]), reads=[Rc], writes=[Rc])
    flg = mk.sb([128, NSEG + 1], F32)
    dma(flg[:], flags_d, writes=[Rc])

    def pload(shape, src):
        t = mk.sb(shape, F32)
        dma(t[:], src, writes=[Rc], allow_slow_non_contiguous=True)
        return t

    g1 = pload([128, DEPTH, 8], W["ln1_g"].rearrange("l (kc p) -> p l kc", p=128))
    g2 = pload([128, DEPTH, 8], W["ln2_g"].rearrange("l (kc p) -> p l kc", p=128))
    cvA = pload([128, DEPTH * 5, 24], W["conv_a"].rearrange("l k (c p) -> p (l k) c", p=128))
    cvF = pload([128, DEPTH * 3, NFF], W["conv_ff"].rearrange("l k (c p) -> p (l k) c", p=128))
    cvFb = pload([128, DEPTH, NFF], W["conv_ff_b"].rearrange("l (c p) -> p l c", p=128))
    nrmA = pload([128, DEPTH], W["norm_a"].rearrange("l p -> p l"))
    qng = mk.sb([128, DEPTH], F32)
    kng = mk.sb([128, DEPTH], F32)
    for hf in range(2):
        dma(qng[hf * 64:(hf + 1) * 64, :], W["qn_b"].rearrange("l p -> p l"), writes=[Rc], allow_slow_non_contiguous=True)
        dma(kng[hf * 64:(hf + 1) * 64, :], W["kn_b"].rearrange("l p -> p l"), writes=[Rc], allow_slow_non_contiguous=True)
    nal = mk.sb([128, DEPTH * 16], F32)
    dtb = mk.sb([128, DEPTH * 16], F32)
    dma(nal[:], bass.AP(tensor=W["a_log"].tensor, offset=0, ap=[[0, 128], [1, DEPTH * 16]]), writes=[Rc])
    dma(dtb[:], bass.AP(tensor=W["dt_bias"].tensor, offset=0, ap=[[0, 128], [1, DEPTH * 16]]), writes=[Rc])
    op("act", lambda e: e.activation(out=nal[:], in_=nal[:], func=AF.Exp), reads=[Rc], writes=[Rc])
    op("dve", lambda e: e.tensor_scalar(out=nal[:], in0=nal[:], scalar1=-1.0, scalar2=None, op0=ALU.mult), reads=[Rc], writes=[Rc])

    RW = {}

    def cast_gen(l, CSTG, CBF):
        for n in ["w_in", "w_oa", "w_ob", "w_out", "w_up", "w_gate", "w_down"]:
            K, N = W[n].shape[1], W[n].shape[2]
            RW[(n, l)] = []
            for r0 in range(0, K, 128):
                for (c0, w) in pieces(N, 2048):
                    st, Rst = CSTG.get()
                    cb_, Rcb = CBF.get()
                    dma(st[:, 0:w], W[n][l, r0:r0 + 128, c0:c0 + w], writes=[Rst])
                    op("pool", lambda e: e.tensor_copy(out=cb_[:, 0:w], in_=st[:, 0:w]), reads=[Rst], writes=[Rcb])
                    R = Res()
                    RW[(n, l)].append(R)
                    dma(WB[n][l, r0:r0 + 128, c0:c0 + w], cb_[:, 0:w], reads=[Rcb], writes=[R], q="pool")
                    yield

    mk.push()
    _g = cast_gen(0, Ring([(mk.sb([128, 2048], F32), Res()) for _ in range(3)]), Ring([(mk.sb([128, 2048], BF16), Res()) for _ in range(3)]))
    for _ in _g:
        pass
    mk.pop()

    zpad = mk.sb([128, 8, 2], F32)
    op("dve", lambda e: e.memset(zpad[:], 0.0), writes=[Rc])
    RX = {"A": Res(), "B": Res()}
    for nm, X in (("A", XA), ("B", XB)):
        for c0 in (0, NT + 2):
            dma(X[:, c0:c0 + 2].rearrange("(kc p) t -> p kc t", p=128), zpad[:], reads=[Rc])

    psA = [(mk.ps([128, 512]), PRes()) for _ in range(2)]
    psS_t = [(mk.ps([128, 512]), PRes()) for _ in range(3)]
    psT_t = [(mk.ps([128, 1024], BF16), PRes()) for _ in range(2)]
    psC_t = (mk.ps([128, 512]), PRes())
    PA = Ring(psA)
    PS = Ring([(psS_t[b][0][:, q * 128:(q + 1) * 128], psS_t[b][1]) for q in range(4) for b in range(3)])
    PT = Ring([(psT_t[b][0][:, q * 128:(q + 1) * 128], psT_t[b][1]) for q in range(4) for b in range(2)])
    PC = Ring([(psC_t[0][:, q * 128:(q + 1) * 128], psC_t[1]) for q in range(4)])

    WP = Ring([(mk.sb([128, 4096], BF16), Res()) for _ in range(3)])

    def wload(n, l, col0, ncols, nk):
        t, R = WP.get()
        v = t[:, 0:nk * ncols].rearrange("p (k c) -> p k c", k=nk)
        dma(v, WB[n][l, :, col0:col0 + ncols].rearrange("(k p) c -> p k c", p=128), reads=RW[(n, l)], writes=[R])
        return v, R

    def mm_acc(ps_ap, Rps, pairs, reads):
        def fn(e):
            n = len(pairs)
            ins = None
            for i, (l_, r_) in enumerate(pairs):
                ins = e.matmul(ps_ap, lhsT=l_, rhs=r_, start=(i == 0), stop=(i == n - 1))
            return ins
        op("pe", fn, reads=reads, writes=[Rps])

    EBIDX = {}
    for g in range(3):
        for hh in range(4):
            for off in range(-OFFS[g], OFFS[g] + 1):
                EBIDX[(g, hh, off)] = len(EBIDX)
    EB = mk.sb([128, len(EBIDX), 128], BF16)
    REB = Res()
    mk.push()
    rb = mk.sb([32, 12], F32)
    ohs = mk.sb([32, NF], F32)
    vls = mk.sb([12, NF], F32)
    fs = mk.sb([12, NF], F32)
    Rs = Res()
    dma(rb[:], W["rel_bias"], writes=[Rs])
    dma(ohs[:], oh_d, writes=[Rs])
    dma(vls[:], valid_d, writes=[Rs])
    for (c0, w) in pieces(NF):
        p, Rp = PA.get()
        mm_acc(p[0:12, 0:w], Rp, [(rb[:], ohs[:, c0:c0 + w])], [Rs])
        op("act", lambda e: e.activation(out=fs[:, c0:c0 + w], in_=p[0:12, 0:w], func=AF.Exp), reads=[Rp], writes=[Rs])
    op("dve", lambda e: e.tensor_tensor(out=fs[:], in0=fs[:], in1=vls[:], op=ALU.mult), reads=[Rs], writes=[Rs])
    RFd = Res()
    dma(Fd, fs[:], reads=[Rs], writes=[RFd])
    hk = Ring([(mk.sb([128, 128], F32), Res()) for _ in range(4)])
    for (g, hh, off), idx in EBIDX.items():
        t, R = hk.get()
        base = 128 * off + (FOFF - 127)
        src = bass.AP(tensor=Fd.tensor, offset=(4 * g + hh) * NF + base, ap=[[1, 128], [1, 128]])
        dma(t[:], src, reads=[RFd], writes=[R])
        op("dve", lambda e: e.tensor_copy(out=EB[:, idx, :], in_=t[:, ::-1]), reads=[R], writes=[REB], acc=True)
    mk.pop()

    def norm_h(hT, Rh, X, RXs, col0, ncols, gain, l, xp, sq, rs, Rt):
        for (c0, w) in pieces(ncols):
            dma(xp[:, :, 0:w], X[:, col0 + c0:col0 + c0 + w].rearrange("(kc p) t -> p kc t", p=128),
                writes=[Rt])
            op("act", lambda e: e.activation(out=sq[:, :, 0:w], in_=xp[:, :, 0:w], func=AF.Square), reads=[Rt], writes=[Rt])
            p, Rp = PA.get()
            mm_acc(p[:, 0:w], Rp, [(ONESB, sq[:, kc, 0:w]) for kc in range(8)], [Rt, Rc])
            op("act", lambda e: e.activation(out=rs[:, 0:w], in_=p[:, 0:w], func=AF.Sqrt, bias=EPS, scale=1.0 / D), reads=[Rp], writes=[Rt])
            op("dve", lambda e: e.reciprocal(out=rs[:, 0:w], in_=rs[:, 0:w]), reads=[Rt], writes=[Rt])
            for kc in range(8):
                op("dve", lambda e: e.scalar_tensor_tensor(out=hT[:, kc, c0:c0 + w], in0=xp[:, kc, 0:w], scalar=gain[:, l, kc:kc + 1],
                                                           in1=rs[:, 0:w], op0=ALU.mult, op1=ALU.mult),
                   reads=[Rt, Rc], writes=[Rh], acc=True)

    for l in range(DEPTH):
        Xin, RXin = (xT, []) if l == 0 else (XB, [RX["B"]])
        Xmid, RXmid = XA, RX["A"]
        last = (l == DEPTH - 1)

        mk.push()
        SW = SEG + 4
        xp = mk.sb([128, 8, 512], F32)
        sq = mk.sb([128, 8, 512], BF16)
        rs = mk.sb([128, 512], F32)
        Rt = Res()
        hT = mk.sb([128, 8, SW], BF16)
        Rh = Res()
        RAW = Ring([(mk.sb([128, SW], F32), Res()) for _ in range(2)])
        acc = mk.sb([128, SEG], F32)
        Racc = Res()
        sqb = mk.sb([128, SEG], BF16)
        RS2 = Ring([(mk.sb([128, 512], F32), Res()) for _ in range(2)])
        OUTB = Ring([(mk.sb([128, SW], BF16), Res()) for _ in range(3)])
        TOK = Ring([(mk.sb([128, 16, 128], BF16), Res()) for _ in range(2)])
        AVS = Ring([(mk.sb([128, 4, 512], BF16), Res()) for _ in range(2)])
        gbs = mk.sb([128, TPS, 32], F32)
        tmpg = mk.sb([128, TPS, 16], F32)
        tmpl = mk.sb([128, TPS, 16], F32)
        Rg = Res()
        Rscr1 = Res()

        for s in range(NSEG):
            t0 = s * SEG
            norm_h(hT, Rh, Xin, RXin, t0, SW, g1, l, xp, sq, rs, Rt)

            PEND = []

            def proj_fm(col0, ncols, handler, post, pcs):
                wt, Rw = wload("w_in", l, col0, ncols, 8)
                for j in range(ncols // 128):
                    st = handler(j, None, None, None, None, init=True)
                    for (c0, w) in pcs:
                        p, Rp = PA.get()
                        mm_acc(p[:, 0:w], Rp, [(wt[:, kc, j * 128:(j + 1) * 128], hT[:, kc, c0:c0 + w]) for kc in range(8)], [Rw, Rh])
                        handler(j, c0, w, p, Rp, st=st)
                    if PEND or not KF_PEND:
                        if not KF_PEND:
                            post(j, st)
                            continue
                        PEND.pop(0)()
                    PEND.append(lambda post=post, j=j, st=st: post(j, st))

            full = pieces(SW)
            inner = [(2 + c0, w) for (c0, w) in pieces(SEG)]

            for grp in range(6):
                def h_raw(j, c0, w, p, Rp, init=False, st=None):
                    if init:
                        return RAW.get()
                    raw, Rr = st
                    op("act", lambda e: e.copy(out=raw[:, c0:c0 + w], in_=p[:, 0:w]), reads=[Rp], writes=[Rr], acc=True)

                def post_a(j, st, grp=grp):
                    raw, Rr = st
                    ci = grp * 4 + j
                    op("dve", lambda e: e.tensor_scalar(out=raw[:, 0:2], in0=raw[:, 0:2], scalar1=flg[:, s:s + 1], scalar2=None, op0=ALU.mult), reads=[Rr, Rc], writes=[Rr])
                    op("dve", lambda e: e.tensor_scalar(out=raw[:, SEG + 2:SEG + 4], in0=raw[:, SEG + 2:SEG + 4], scalar1=flg[:, s + 1:s + 2], scalar2=None, op0=ALU.mult), reads=[Rr, Rc], writes=[Rr])
                    op("act", lambda e: e.activation(out=acc[:], in_=raw[:, 0:SEG], func=AF.Copy, scale=cvA[:, l * 5, ci:ci + 1]), reads=[Rr, Rc], writes=[Racc])
                    for k in range(1, 5):
                        op("dve", lambda e: e.scalar_tensor_tensor(out=acc[:], in0=raw[:, k:k + SEG], scalar=cvA[:, l * 5 + k, ci:ci + 1], in1=acc[:], op0=ALU.mult, op1=ALU.add), reads=[Rr, Rc, Racc], writes=[Racc])
                    ob, Ro = OUTB.get()
                    if ci < 16:
                        op("act", lambda e: e.activation(out=acc[:], in_=acc[:], func=AF.Silu), reads=[Racc], writes=[Racc])
                        op("act", lambda e: e.activation(out=sqb[:], in_=acc[:], func=AF.Square), reads=[Racc], writes=[Racc])
                        for (c0, w) in pieces(SEG):
                            p, Rp = PA.get()
                            mm_acc(p[:, 0:w], Rp, [(ONESB, sqb[:, c0:c0 + w])], [Racc, Rc])
                            r2, Rr2 = RS2.get()
                            op("act", lambda e: e.activation(out=r2[:, 0:w], in_=p[:, 0:w], func=AF.Sqrt, bias=EPS, scale=1.0), reads=[Rp], writes=[Rr2])
                            op("dve", lambda e: e.reciprocal(out=r2[:, 0:w], in_=r2[:, 0:w]), reads=[Rr2], writes=[Rr2])
                            sc = (128.0 ** -0.5) if ci < 8 else 1.0
                            op("dve", lambda e: e.scalar_tensor_tensor(out=ob[:, c0:c0 + w], in0=acc[:, c0:c0 + w], scalar=sc, in1=r2[:, 0:w], op0=ALU.mult, op1=ALU.mult), reads=[Racc, Rr2], writes=[Ro], acc=True)
                        dst = cq if ci < 8 else ck
                        dma(dst[ci % 8, :, t0:t0 + SEG], ob[:, 0:SEG], reads=[Ro], q="pool", acc=True)
                    else:
                        op("act", lambda e: e.activation(out=ob[:, 0:SEG], in_=acc[:], func=AF.Silu), reads=[Racc], writes=[Ro])
                    if ci >= 8:
                        tk, Rtk = TOK.get()
                        for jt in range(TPS):
                            pt, Rpt = PT.get()
                            op("pe", lambda e: e.transpose(out=pt, in_=ob[:, jt * 128:(jt + 1) * 128], identity=IDB), reads=[Ro, Rc], writes=[Rpt])
                            eng = "act" if jt % 2 == 0 else "dve"
                            if eng == "act":
                                op("act", lambda e: e.copy(out=tk[:, jt, :], in_=pt), reads=[Rpt], writes=[Rtk], acc=True)
                            else:
                                op("dve", lambda e: e.tensor_copy(out=tk[:, jt, :], in_=pt), reads=[Rpt], writes=[Rtk], acc=True)
                        dst = kt if ci < 16 else vt
                        hcol = (ci % 8) * 128
                        dma(dst[t0:t0 + SEG, hcol:hcol + 128].rearrange("(j p) d -> p j d", p=128), tk[:, 0:TPS, :], reads=[Rtk], q="pool", acc=True)

                proj_fm(grp * 512, 512, h_raw, post_a, full)

            def act_group(col0, func, dst, row0):
                def h(j, c0, w, p, Rp, init=False, st=None):
                    if init:
                        return OUTB.get()
                    ob, Ro = st
                    op("act", lambda e: e.activation(out=ob[:, c0:c0 + w], in_=p[:, 0:w], func=func), reads=[Rp], writes=[Ro], acc=True)

                def post(j, st):
                    ob, Ro = st
                    dma(dst[row0 + j * 128:row0 + (j + 1) * 128, t0:t0 + SEG], ob[:, 2:2 + SEG], reads=[Ro], q="pool", acc=True)
                proj_fm(col0, 512, h, post, inner)

            for grp in range(2):
                act_group(OFF_Z + grp * 512, AF.Silu, zt, grp * 512)
            for grp in range(4):
                act_group(OFF_GATE + grp * 512, AF.Sigmoid, gt, grp * 512)

            for (col0, dst, gn) in ((OFF_QB, aq, qng), (OFF_KB, ak, kng)):
                for (cc, ncols) in ((0, 512), (512, 256)):
                    def h_raw2(j, c0, w, p, Rp, init=False, st=None):
                        if init:
                            return RAW.get()
                        raw, Rr = st
                        op("act", lambda e: e.copy(out=raw[:, c0:c0 + w], in_=p[:, 0:w]), reads=[Rp], writes=[Rr], acc=True)

                    def post_b(j, st, cc=cc, dst=dst, gn=gn):
                        raw, Rr = st
                        ob, Ro = OUTB.get()
                        op("act", lambda e: e.activation(out=sqb[:], in_=raw[:, 2:2 + SEG], func=AF.Square), reads=[Rr], writes=[Racc])
                        for (c0, w) in pieces(SEG):
                            p, Rp = PA.get()
                            mm_acc(p[:, 0:w], Rp, [(BD64B, sqb[:, c0:c0 + w])], [Racc, Rc])
                            r2, Rr2 = RS2.get()
                            op("act", lambda e: e.activation(out=r2[:, 0:w], in_=p[:, 0:w], func=AF.Sqrt, bias=EPS, scale=1.0), reads=[Rp], writes=[Rr2])
                            op("dve", lambda e: e.reciprocal(out=r2[:, 0:w], in_=r2[:, 0:w]), reads=[Rr2], writes=[Rr2])
                            op("dve", lambda e: e.scalar_tensor_tensor(out=ob[:, c0:c0 + w], in0=raw[:, 2 + c0:2 + c0 + w], scalar=gn[:, l:l + 1], in1=r2[:, 0:w], op0=ALU.mult, op1=ALU.mult), reads=[Rr, Rr2, Rc], writes=[Ro], acc=True)
                        r0 = cc + j * 128
                        dma(dst[r0:r0 + 128, t0:t0 + SEG], ob[:, 0:SEG], reads=[Ro], q="pool", acc=True)
                    proj_fm(col0 + cc, ncols, h_raw2, post_b, inner)

            while PEND:
                PEND.pop(0)()
            for (cc, ncols) in ((0, 512), (512, 256)):
                wt, Rw = wload("w_in", l, OFF_VB + cc, ncols, 8)
                for q4 in range(TPS // 4):
                    sv, Rsv = AVS.get()
                    for jj in range(4):
                        jt = q4 * 4 + jj
                        p, Rp = PA.get()
                        mm_acc(p[:, 0:ncols], Rp, [(hT[:, kc, 2 + jt * 128:2 + (jt + 1) * 128], wt[:, kc, :]) for kc in range(8)], [Rw, Rh])
                        if jj % 2 == 0:
                            op("act", lambda e: e.copy(out=sv[:, jj, 0:ncols], in_=p[:, 0:ncols]), reads=[Rp], writes=[Rsv], acc=True)
                        else:
                            op("dve", lambda e: e.tensor_copy(out=sv[:, jj, 0:ncols], in_=p[:, 0:ncols]), reads=[Rp], writes=[Rsv], acc=True)
                    r0 = t0 + q4 * 512
                    dma(av[r0:r0 + 512, cc:cc + ncols].rearrange("(j p) d -> p j d", p=128), sv[:, :, 0:ncols], reads=[Rsv], q="pool", acc=True)
            wt, Rw = wload("w_in", l, OFF_BETA, 32, 8)
            p, Rp = PA.get()
            for jt in range(TPS):
                mm_acc(p[:, jt * 32:(jt + 1) * 32], Rp, [(hT[:, kc, 2 + jt * 128:2 + (jt + 1) * 128], wt[:, kc, :]) for kc in range(8)], [Rw, Rh])
            pv = p[:, 0:TPS * 32].rearrange("p (j c) -> p j c", c=32)
            op("act", lambda e: e.activation(out=gbs[:, :, 16:32], in_=pv[:, :, 0:16], func=AF.Sigmoid), reads=[Rp], writes=[Rg])
            for jt in range(TPS):
                op("dve", lambda e: e.tensor_tensor(out=tmpg[:, jt, :], in0=pv[:, jt, 16:32], in1=dtb[:, l * 16:(l + 1) * 16], op=ALU.add), reads=[Rp, Rc, Rg], writes=[Rg])
            op("act", lambda e: e.activation(out=tmpl[:], in_=tmpg[:], func=AF.Abs), reads=[Rg], writes=[Rg])
            op("act", lambda e: e.activation(out=tmpl[:], in_=tmpl[:], func=AF.Exp, scale=-1.0), reads=[Rg], writes=[Rg])
            op("act", lambda e: e.activation(out=tmpl[:], in_=tmpl[:], func=AF.Ln, bias=1.0), reads=[Rg], writes=[Rg])
            op("dve", lambda e: e.scalar_tensor_tensor(out=tmpg[:], in0=tmpg[:], scalar=0.0, in1=tmpl[:], op0=ALU.max, op1=ALU.add), reads=[Rg], writes=[Rg])
            for jt in range(TPS):
                op("dve", lambda e: e.tensor_tensor(out=gbs[:, jt, 0:16], in0=tmpg[:, jt, :], in1=nal[:, l * 16:(l + 1) * 16], op=ALU.mult), reads=[Rg, Rc], writes=[Rg])
            dma(gbd[t0:t0 + SEG, :].rearrange("(j p) c -> p j c", p=128), gbs[:], reads=[Rg], q="pool", acc=True)
        mk.pop()

        mk.push()
        NB = 2
        qT8 = [(mk.sb([128, 8, 128], BF16), Res()) for _ in range(NB)]
        kT8 = [(mk.sb([128, 8, 128], BF16), Res()) for _ in range(NB)]
        ktk = [(mk.sb([128, D], BF16), Res()) for _ in range(NB)]
        vtk = [(mk.sb([128, D], BF16), Res()) for _ in range(NB)]
        gbt = [(mk.sb([128, 32], F32), Res()) for _ in range(NB)]
        oft = [(mk.sb([128, D], F32), Res()) for _ in range(NB)]
        zs8 = [(mk.sb([128, 8, 128], BF16), Res()) for _ in range(NB)]
        sm = [(mk.sb([128, 6, 16], F32), Res()) for _ in range(NB)]
        gr_ = [mk.sb([128, 8], F32R) for _ in range(NB)]
        named = [[{k: (mk.sb([128, 128], BF16), Res()) for k in ("MT", "R", "bv", "kd")} for h in range(8)] for _ in range(NB)]
        TB = Ring([(mk.sb([128, 128], BF16), Res()) for _ in range(40)])
        TF = Ring([(mk.sb([128, 128], F32), Res()) for _ in range(24)])
        TR = Ring([(mk.sb([128, 128], F32R), Res()) for _ in range(40)])
        Sst = [(mk.sb([128, 128], F32), Res()) for _ in range(8)]
        Sbf = [(mk.sb([128, 128], BF16), Res()) for _ in range(8)]
        ostg = [(mk.sb([128, 8, 128], F32), Res()) for _ in range(2)]
        ogst = [(mk.sb([128, 8, 128], BF16), Res()) for _ in range(2)]
        ssq = [(mk.sb([128, 16], F32), Res()) for _ in range(2)]
        junk = mk.sb([128, 128], BF16)
        Rj = Res()
        Rof = Res()
        Roag = Res()

        for dr in range(2):
            chunk_order = list(range(NTILE)) if dr == 0 else list(range(NTILE - 1, -1, -1))
            pmA = PM_L_S if dr == 0 else PM_U_S
            pmT = PM_U_I if dr == 0 else PM_L_I
            CUM = UF if dr == 0 else LF
            CUMR = cr[:, 0:128] if dr == 0 else cr[:, 128:256]

            def load_chunk(ci_, b):
                c = chunk_order[ci_]
                tk0 = c * 128
                dma(qT8[b][0][:], cq[:, :, tk0:tk0 + 128].rearrange("h d t -> d h t"), writes=[qT8[b][1]])
                dma(kT8[b][0][:], ck[:, :, tk0:tk0 + 128].rearrange("h d t -> d h t"), writes=[kT8[b][1]])
                dma(ktk[b][0][:], kt[tk0:tk0 + 128, :], writes=[ktk[b][1]])
                dma(vtk[b][0][:], vt[tk0:tk0 + 128, :], writes=[vtk[b][1]])
                dma(gbt[b][0][:], gbd[tk0:tk0 + 128, :], writes=[gbt[b][1]])
                if dr == 1:
                    dma(oft[b][0][:], ofd[tk0:tk0 + 128, :], writes=[oft[b][1]])
                    dma(zs8[b][0][:], zt[:, tk0:tk0 + 128].rearrange("(h d) t -> d h t", d=128), writes=[zs8[b][1]])

            load_chunk(0, 0)
            for ci_ in range(NTILE):
                b = ci_ % NB
                c = chunk_order[ci_]
                s = c // TPS
                if ci_ + 1 < NTILE:
                    load_chunk(ci_ + 1, (ci_ + 1) % NB)
                q8, Rq8 = qT8[b]
                k8, Rk8 = kT8[b]
                kk_, Rkk = ktk[b]
                vv_, Rvv = vtk[b]
                gb_, Rgb = gbt[b]
                smt, Rsm = sm[b]
                GC, EG, EKD, NBE, EGL, TMP = (smt[:, i, :] for i in range(6))
                g0 = dr * 8
                grr = gr_[b]
                op("dve", lambda e: e.tensor_copy(out=grr[:], in_=gb_[:, g0:g0 + 8]), reads=[Rgb, Rsm], writes=[Rsm])
                p, Rp = PS.get()
                op("pe", lambda e: e.matmul(p[:, 0:8], lhsT=CUM, rhs=gb_[:, g0:g0 + 8], start=True, stop=True), reads=[Rgb, Rc], writes=[Rp])
                op("pe", lambda e: e.matmul(p[:, 8:16], lhsT=ONESF, rhs=gb_[:, g0:g0 + 8], start=True, stop=True), reads=[Rgb, Rc], writes=[Rp], acc=True)
                op("act", lambda e: e.copy(out=GC[:, 0:8], in_=p[:, 0:8]), reads=[Rp], writes=[Rsm])
                op("act", lambda e: e.activation(out=EG[:, 0:8], in_=p[:, 0:8], func=AF.Exp), reads=[Rp], writes=[Rsm], acc=True)
                op("act", lambda e: e.activation(out=EGL[:, 0:8], in_=p[:, 8:16], func=AF.Exp), reads=[Rp], writes=[Rsm], acc=True)
                op("dve", lambda e: e.tensor_tensor(out=TMP[:, 0:8], in0=p[:, 8:16], in1=GC[:, 0:8], op=ALU.subtract), reads=[Rp, Rsm], writes=[Rsm])
                op("act", lambda e: e.activation(out=EKD[:, 0:8], in_=TMP[:, 0:8], func=AF.Exp), reads=[Rsm], writes=[Rsm])
                op("dve", lambda e: e.scalar_tensor_tensor(out=NBE[:, 0:8], in0=gb_[:, 16 + g0:24 + g0], scalar=-1.0, in1=EG[:, 0:8], op0=ALU.mult, op1=ALU.mult), reads=[Rgb, Rsm], writes=[Rsm])
                first_in_seg = (c % TPS == 0) if dr == 0 else (c % TPS == TPS - 1)
                if first_in_seg:
                    fcol = s if dr == 0 else s + 1
                    for h in range(8):
                        if ci_ == 0:
                            op("dve", lambda e: e.memset(Sst[h][0][:], 0.0), writes=[Sst[h][1]])
                            op("pool", lambda e: e.memset(Sbf[h][0][:], 0.0), writes=[Sbf[h][1]])
                        else:
                            op("dve", lambda e: e.tensor_scalar(out=Sst[h][0][:], in0=Sst[h][0][:], scalar1=flg[:, fcol:fcol + 1], scalar2=None, op0=ALU.mult), reads=[Sst[h][1], Rc], writes=[Sst[h][1]])
                            op("pool", lambda e: e.tensor_scalar(out=Sbf[h][0][:], in0=Sbf[h][0][:], scalar1=flg[:, fcol:fcol + 1], scalar2=None, op0=ALU.mult), reads=[Sbf[h][1], Rc], writes=[Sbf[h][1]])
                for hg in range(2):
                    HS = list(range(hg * 4, hg * 4 + 4))
                    beta = {h: gb_[:, 16 + g0 + h:17 + g0 + h] for h in HS}
                    pkk, pqk, pgr, t1, t2, dec, decT, A_, B_, ptb, Rm = ({} for _ in range(11))
                    for h in HS:
                        kTh = k8[:, h, :]
                        pkk[h] = PS.get()
                        mm_acc(pkk[h][0], pkk[h][1], [(kTh, kTh)], [Rk8])
                        pqk[h] = PS.get()
                        mm_acc(pqk[h][0], pqk[h][1], [(kTh, q8[:, h, :])], [Rk8, Rq8])
                        pgr[h] = PS.get()
                        mm_acc(pgr[h][0], pgr[h][1], [(grr[:, h:h + 1].to_broadcast([128, 128]), CUMR)], [Rsm, Rc])
                    for h in HS:
                        t1[h] = TF.get()
                        t2[h] = TF.get()
                        op("dve", lambda e: e.scalar_tensor_tensor(out=t1[h][0][:], in0=pgr[h][0], scalar=GC[:, h:h + 1], in1=pmA, op0=ALU.subtract, op1=ALU.add), reads=[pgr[h][1], Rsm, Rc], writes=[t1[h][1]])
                        op("dve", lambda e: e.scalar_tensor_tensor(out=t2[h][0][:], in0=pgr[h][0], scalar=GC[:, h:h + 1], in1=pmT, op0=ALU.subtract, op1=ALU.subtract), reads=[pgr[h][1], Rsm, Rc], writes=[t2[h][1]])
                    for h in HS:
                        dec[h] = TF.get()
                        decT[h] = TF.get()
                        op("act", lambda e: e.activation(out=dec[h][0][:], in_=t1[h][0][:], func=AF.Exp, scale=-1.0), reads=[t1[h][1]], writes=[dec[h][1]])
                        op("act", lambda e: e.activation(out=decT[h][0][:], in_=t2[h][0][:], func=AF.Exp), reads=[t2[h][1]], writes=[decT[h][1]])
                    for h in HS:
                        A_[h] = TR.get()
                        op("dve", lambda e: e.scalar_tensor_tensor(out=A_[h][0][:], in0=pkk[h][0], scalar=beta[h], in1=dec[h][0][:], op0=ALU.mult, op1=ALU.mult), reads=[pkk[h][1], Rgb, dec[h][1]], writes=[A_[h][1]])
                        MT, RMT = named[b][h]["MT"]
                        op("dve", lambda e: e.tensor_tensor(out=MT[:], in0=pqk[h][0], in1=decT[h][0][:], op=ALU.mult), reads=[pqk[h][1], decT[h][1]], writes=[RMT])
                    for h in HS:
                        ptb[h] = PS.get()
                        op("pe", lambda e: e.matmul(ptb[h][0], lhsT=A_[h][0][:], rhs=cr[:, 256:384], start=True, stop=True), reads=[A_[h][1], Rc], writes=[ptb[h][1]])
                    for h in HS:
                        B_[h] = TR.get()
                        op("act", lambda e: e.copy(out=B_[h][0][:], in_=ptb[h][0]), reads=[ptb[h][1]], writes=[B_[h][1]])
                    for h in HS:
                        Rm[h] = TR.get()
                        op("pool", lambda e: e.tensor_tensor(out=Rm[h][0][:], in0=IDF, in1=B_[h][0][:], op=ALU.subtract), reads=[B_[h][1], Rc], writes=[Rm[h][1]])
                    P_ = dict(A_)
                    Q_ = dict(B_)
                    for k in range(1, 7):
                        pp, pq, pr, Pn, Qn, Rn = ({} for _ in range(6))
                        for h in HS:
                            pp[h] = PS.get()
                            mm_acc(pp[h][0], pp[h][1], [(Q_[h][0][:], P_[h][0][:])], [Q_[h][1], P_[h][1]])
                            if k < 6:
                                pq[h] = PS.get()
                                mm_acc(pq[h][0], pq[h][1], [(P_[h][0][:], Q_[h][0][:])], [Q_[h][1], P_[h][1]])
                        for h in HS:
                            Pn[h] = TR.get()
                            op("act", lambda e: e.copy(out=Pn[h][0][:], in_=pp[h][0]), reads=[pp[h][1]], writes=[Pn[h][1]])
                            if k < 6:
                                Qn[h] = TR.get()
                                op("dve", lambda e: e.tensor_copy(out=Qn[h][0][:], in_=pq[h][0]), reads=[pq[h][1]], writes=[Qn[h][1]])
                        for h in HS:
                            pr[h] = PS.get()
                            mm_acc(pr[h][0], pr[h][1], [(Pn[h][0][:], Rm[h][0][:])], [Pn[h][1], Rm[h][1]])
                        for h in HS:
                            Rn[h] = TR.get() if k < 6 else named[b][h]["R"]
                            op("dve", lambda e: e.tensor_tensor(out=Rn[h][0][:], in0=pr[h][0], in1=Rm[h][0][:], op=ALU.add), reads=[pr[h][1], Rm[h][1]], writes=[Rn[h][1]])
                        Rm = Rn
                        P_ = Pn
                        if k < 6:
                            Q_ = Qn
                    for h in HS:
                        bv, Rbv = named[b][h]["bv"]
                        op("act", lambda e: e.activation(out=bv[:], in_=vv_[:, h * 128:(h + 1) * 128], func=AF.Copy, scale=beta[h]), reads=[Rvv, Rgb], writes=[Rbv])
                        kd, Rkd = named[b][h]["kd"]
                        op("act", lambda e: e.activation(out=kd[:], in_=kk_[:, h * 128:(h + 1) * 128], func=AF.Copy, scale=EKD[:, h:h + 1]), reads=[Rkk, Rsm], writes=[Rkd])
                ost, Rost = ostg[ci_ % 2]
                pks, pvn, pqs, pmv, vnw, rr = {}, {}, {}, {}, {}, {}
                for h in range(8):
                    pks[h] = PS.get()
                    mm_acc(pks[h][0], pks[h][1], [(k8[:, h, :], Sbf[h][0][:])], [Rk8, Sbf[h][1]])
                for h in range(8):
                    rr[h] = TB.get()
                    op("dve", lambda e: e.scalar_tensor_tensor(out=rr[h][0][:], in0=pks[h][0], scalar=NBE[:, h:h + 1], in1=named[b][h]["bv"][0][:], op0=ALU.mult, op1=ALU.add),
                       reads=[pks[h][1], Rsm, named[b][h]["bv"][1]], writes=[rr[h][1]])
                for h in range(8):
                    pvn[h] = PS.get()
                    mm_acc(pvn[h][0], pvn[h][1], [(named[b][h]["R"][0][:], rr[h][0][:])], [named[b][h]["R"][1], rr[h][1]])
                for h in range(8):
                    vnw[h] = TB.get()
                    op("act", lambda e: e.copy(out=vnw[h][0][:], in_=pvn[h][0]), reads=[pvn[h][1]], writes=[vnw[h][1]])
                for h in range(8):
                    pqs[h] = PS.get()
                    mm_acc(pqs[h][0], pqs[h][1], [(q8[:, h, :], Sbf[h][0][:])], [Rq8, Sbf[h][1]])
                    tq, Rtq = TF.get()
                    op("act", lambda e: e.activation(out=tq[:], in_=pqs[h][0], func=AF.Copy, scale=EG[:, h:h + 1]), reads=[pqs[h][1], Rsm], writes=[Rtq])
                    pmv[h] = PS.get()
                    mm_acc(pmv[h][0], pmv[h][1], [(named[b][h]["MT"][0][:], vnw[h][0][:])], [named[b][h]["MT"][1], vnw[h][1]])
                    op("dve", lambda e: e.tensor_tensor(out=ost[:, h, :], in0=pmv[h][0], in1=tq[:], op=ALU.add), reads=[pmv[h][1], Rtq], writes=[Rost], acc=(h > 0))
                    psu, Rpsu = PS.get()
                    mm_acc(psu, Rpsu, [(named[b][h]["kd"][0][:], vnw[h][0][:])], [named[b][h]["kd"][1], vnw[h][1]])
                    op("dve", lambda e: e.scalar_tensor_tensor(out=Sst[h][0][:], in0=Sst[h][0][:], scalar=EGL[:, h:h + 1], in1=psu, op0=ALU.mult, op1=ALU.add), reads=[Rpsu, Rsm, Sst[h][1]], writes=[Sst[h][1]])
                    op("act", lambda e: e.copy(out=Sbf[h][0][:], in_=Sst[h][0][:]), reads=[Sst[h][1]], writes=[Sbf[h][1]])
                tk0 = c * 128
                if dr == 0:
                    dma(ofd[tk0:tk0 + 128, :], ost[:].rearrange("p h d -> p (h d)"), reads=[Rost], q="pool", acc=True)
                else:
                    of_, Rof_ = oft[b]
                    z8, Rz8 = zs8[b]
                    ogs, Rogs = ogst[ci_ % 2]
                    sq_, Rsq = ssq[ci_ % 2]
                    op("pool", lambda e: e.tensor_tensor(out=ost[:].rearrange("p h d -> p (h d)"), in0=ost[:].rearrange("p h d -> p (h d)"), in1=of_[:], op=ALU.add), reads=[Rost, Rof_], writes=[Rost])
                    op("dve", lambda e: e.memset(sq_[:], 0.0), writes=[Rsq])
                    for h in range(8):
                        op("act", lambda e: e.activation(out=junk[:], in_=ost[:, h, :], func=AF.Square, accum_out=sq_[:, h:h + 1]), reads=[Rost, Rsq], writes=[Rj, Rsq])
                    op("act", lambda e: e.activation(out=sq_[:, 8:16], in_=sq_[:, 0:8], func=AF.Sqrt, bias=EPS, scale=1.0 / 128), reads=[Rsq], writes=[Rsq])
                    op("dve", lambda e: e.reciprocal(out=sq_[:, 8:16], in_=sq_[:, 8:16]), reads=[Rsq], writes=[Rsq])
                    for h in range(8):
                        on, Ron = TB.get()
                        op("act", lambda e: e.activation(out=on[:], in_=ost[:, h, :], func=AF.Copy, scale=sq_[:, 8 + h:9 + h]), reads=[Rost, Rsq], writes=[Ron])
                        pt, Rpt = PT.get()
                        op("pe", lambda e: e.transpose(out=pt, in_=on[:], identity=IDB), reads=[Ron, Rc], writes=[Rpt])
                        op("dve", lambda e: e.scalar_tensor_tensor(out=ogs[:, h, :], in0=pt, scalar=nrmA[:, l:l + 1], in1=z8[:, h, :], op0=ALU.mult, op1=ALU.mult), reads=[Rpt, Rc, Rz8], writes=[Rogs], acc=(h > 0))
                    dma(oag[:, tk0:tk0 + 128].rearrange("(h d) t -> d h t", d=128), ogs[:], reads=[Rogs], q="pool", acc=True)
            mk.barrier()
        mk.pop()

        mk.push()
        WIN = 1024
        KW = SEG + 2 * WIN
        kTw = [(mk.sb([64, KW], BF16), Res()) for _ in range(3)]
        qTw = [(mk.sb([64, SEG], BF16), Res()) for _ in range(3)]
        vaw = [(mk.sb([128, KW // 128, 65], BF16), Res()) for _ in range(3)]
        for g in range(3):
            op("dve", lambda e: e.memset(vaw[g][0][:], 1.0), writes=[vaw[g][1]])
        ET = Ring([(mk.sb([128, 128], BF16), Res()) for _ in range(8)])
        PTl = Ring([(mk.sb([128, 128], BF16), Res()) for _ in range(8)])
        PC = Ring([(psC_t[0][:, 0:128], psC_t[1]), (psA[0][0][:, 0:128], psA[0][1]), (psA[1][0][:, 0:128], psA[1][1])])
        obuf = mk.sb([128, TPS, 256], BF16)
        Rob = Res()
        obT = [(mk.sb([128, 2, SEG], BF16), Res()) for _ in range(2)]
        rden = Ring([(mk.sb([128, 1], F32), Res()) for _ in range(4)])
        Robd = Res()
        cgen = None
        if l + 1 < DEPTH:
            cgen = cast_gen(l + 1, Ring([(mk.sb([128, 2048], F32), Res()) for _ in range(2)]), Ring([(mk.sb([128, 2048], BF16), Res()) for _ in range(2)]))
        for s in range(NSEG):
            t0 = s * SEG
            lo = max(0, t0 - WIN)
            hi = min(NT, t0 + SEG + WIN)
            w0 = lo - (t0 - WIN)
            for hh in range(4):
                for g in range(3):
                    hd = 4 * g + hh
                    dma(kTw[g][0][:, w0:w0 + hi - lo], ak[hd * 64:(hd + 1) * 64, lo:hi], writes=[kTw[g][1]])
                    dma(qTw[g][0][:], aq[hd * 64:(hd + 1) * 64, t0:t0 + SEG], writes=[qTw[g][1]])
                    dma(vaw[g][0][:, w0 // 128:(w0 + hi - lo) // 128, 0:64], av[lo:hi, hd * 64:(hd + 1) * 64].rearrange("(j p) d -> p j d", p=128), writes=[vaw[g][1]])
                if cgen is not None:
                    for _ in range(-(-120 // (NSEG * 4))):
                        next(cgen, None)
                steps_all = []
                for qt in range(TPS):
                    qabs = s * TPS + qt
                    steps = []
                    for g in range(3):
                        for off in range(-OFFS[g], OFFS[g] + 1):
                            kabs = qabs + off
                            if kabs < 0 or kabs >= NTILE:
                                continue
                            sk = kabs // TPS
                            fl = None if sk == s else (s if sk < s else s + 1)
                            steps.append((g, off, kabs, fl))
                    for i, st_ in enumerate(steps):
                        steps_all.append((qt, i, len(steps)) + st_)
                LOOK = KF_LOOK
                ptiles = {}
                cur = {}
                for idx in range(len(steps_all) + LOOK):
                    if idx < len(steps_all):
                        qt, i, n, g, off, kabs, fl = steps_all[idx]
                        kc0 = kabs * 128 - (t0 - WIN)
                        pS, RpS = PS.get()
                        mm_acc(pS, RpS, [(kTw[g][0][:, kc0:kc0 + 128], qTw[g][0][:, qt * 128:(qt + 1) * 128])], [kTw[g][1], qTw[g][1]])
                        E_, RE = ET.get()
                        op("act", lambda e: e.activation(out=E_[:], in_=pS, func=AF.Exp, scale=0.125), reads=[RpS], writes=[RE])
                        P_, RP = PTl.get()
                        ebt = EB[:, EBIDX[(g, hh, off)], :]
                        if fl is None:
                            op("dve", lambda e: e.tensor_tensor(out=P_[:], in0=E_[:], in1=ebt, op=ALU.mult), reads=[RE, REB], writes=[RP])
                        else:
                            op("dve", lambda e: e.scalar_tensor_tensor(out=P_[:], in0=E_[:], scalar=flg[:, fl:fl + 1], in1=ebt, op0=ALU.mult, op1=ALU.mult), reads=[RE, REB, Rc], writes=[RP])
                        ptiles[idx] = (P_, RP)
                    if idx >= LOOK:
                        qt, i, n, g, off, kabs, fl = steps_all[idx - LOOK]
                        P_, RP = ptiles.pop(idx - LOOK)
                        kc0 = kabs * 128 - (t0 - WIN)
                        if i == 0:
                            cur["pc"] = PC.get()
                        pc, Rpc = cur["pc"]
                        va = vaw[g][0][:, kc0 // 128, :]
                        op("pe", lambda e: e.matmul(pc[:, 0:65], lhsT=P_[:], rhs=va, start=(i == 0), stop=(i == n - 1)), reads=[RP, vaw[g][1]], writes=[Rpc], acc=(i > 0))
                        if i == n - 1:
                            rd, Rrd = rden.get()
                            op("dve", lambda e: e.reciprocal(out=rd[:], in_=pc[:, 64:65]), reads=[Rpc], writes=[Rrd])
                            op("act", lambda e: e.activation(out=obuf[:, qt, hh * 64:(hh + 1) * 64], in_=pc[:, 0:64], func=AF.Copy, scale=rd[:, 0:1]), reads=[Rpc, Rrd], writes=[Rob], acc=True)
            oT, RoT = obT[s % 2]
            for qt in range(TPS):
                for hf in range(2):
                    pt, Rpt = PT.get()
                    op("pe", lambda e: e.transpose(out=pt, in_=obuf[:, qt, hf * 128:(hf + 1) * 128], identity=IDB), reads=[Rob, Rc], writes=[Rpt])
                    if hf == 0:
                        op("act", lambda e: e.copy(out=oT[:, hf, qt * 128:(qt + 1) * 128], in_=pt), reads=[Rpt], writes=[RoT], acc=True)
                    else:
                        op("dve", lambda e: e.tensor_copy(out=oT[:, hf, qt * 128:(qt + 1) * 128], in_=pt), reads=[Rpt], writes=[RoT], acc=True)
            dma(obd[:, t0:t0 + SEG].rearrange("(k p) t -> p k t", p=128), oT[:], reads=[RoT], q="pool", acc=True)
        if cgen is not None:
            for _ in cgen:
                pass
        mk.pop()

        mk.push()
        oaT = mk.sb([128, 8, SEG], BF16)
        obT5 = mk.sb([128, 2, SEG], BF16)
        Rin5 = Res()
        mixed = mk.sb([128, 8, SEG], BF16)
        Rmx = Res()
        GT = Ring([(mk.sb([128, 2, 512], BF16), Res()) for _ in range(3)])
        M1 = Ring([(mk.sb([128, 512], F32), Res()) for _ in range(2)])
        M2 = Ring([(mk.sb([128, 512], F32), Res()) for _ in range(2)])
        XT = Ring([(mk.sb([128, 512], F32), Res()) for _ in range(3)])
        XO = Ring([(mk.sb([128, 512], F32), Res()) for _ in range(3)])
        for s in range(NSEG):
            t0 = s * SEG
            dma(oaT[:], oag[:, t0:t0 + SEG].rearrange("(k p) t -> p k t", p=128), writes=[Rin5])
            dma(obT5[:], obd[:, t0:t0 + SEG].rearrange("(k p) t -> p k t", p=128), writes=[Rin5], acc=True)
            for grp in range(2):
                wa, Rwa = wload("w_oa", l, grp * 512, 512, 8)
                wb_, Rwb = wload("w_ob", l, grp * 512, 512, 2)
                for j in range(4):
                    c = grp * 4 + j
                    for (c0, w) in pieces(SEG):
                        gtile, Rgt = GT.get()
                        dma(gtile[:, 0, :], gt[c * 128:(c + 1) * 128, t0 + c0:t0 + c0 + w], writes=[Rgt])
                        dma(gtile[:, 1, :], gt[1024 + c * 128:1024 + (c + 1) * 128, t0 + c0:t0 + c0 + w], writes=[Rgt], acc=True)
                        p1, Rp1 = PA.get()
                        mm_acc(p1[:, 0:w], Rp1, [(wa[:, k, j * 128:(j + 1) * 128], oaT[:, k, c0:c0 + w]) for k in range(8)], [Rwa, Rin5])
                        p2, Rp2 = PA.get()
                        mm_acc(p2[:, 0:w], Rp2, [(wb_[:, k, j * 128:(j + 1) * 128], obT5[:, k, c0:c0 + w]) for k in range(2)], [Rwb, Rin5])
                        m1, Rm1 = M1.get()
                        m2, Rm2 = M2.get()
                        op("dve", lambda e: e.tensor_tensor(out=m1[:], in0=p1[:, 0:w], in1=gtile[:, 0, :], op=ALU.mult), reads=[Rp1, Rgt], writes=[Rm1])
                        op("dve", lambda e: e.tensor_tensor(out=m2[:], in0=p2[:, 0:w], in1=gtile[:, 1, :], op=ALU.mult), reads=[Rp2, Rgt], writes=[Rm2])
                        op("pool", lambda e: e.tensor_tensor(out=mixed[:, c, c0:c0 + w], in0=m1[:], in1=m2[:], op=ALU.add), reads=[Rm1, Rm2], writes=[Rmx], acc=True)
            for grp in range(2):
                wo, Rwo = wload("w_out", l, grp * 512, 512, 8)
                for j in range(4):
                    c = grp * 4 + j
                    for (c0, w) in pieces(SEG):
                        xt_, Rxt = XT.get()
                        dma(xt_[:], Xin[c * 128:(c + 1) * 128, 2 + t0 + c0:2 + t0 + c0 + w], writes=[Rxt])
                        p1, Rp1 = PA.get()
                        mm_acc(p1[:, 0:w], Rp1, [(wo[:, k, j * 128:(j + 1) * 128], mixed[:, k, c0:c0 + w]) for k in range(8)], [Rwo, Rmx])
                        xo, Rxo = XO.get()
                        op("dve", lambda e: e.tensor_tensor(out=xo[:], in0=p1[:, 0:w], in1=xt_[:], op=ALU.add), reads=[Rp1, Rxt], writes=[Rxo])
                        dma(Xmid[c * 128:(c + 1) * 128, 2 + t0 + c0:2 + t0 + c0 + w], xo[:], reads=[Rxo], q="pool", acc=True)
        mk.pop()

        mk.push()
        FW = FB + 2
        xp = mk.sb([128, 8, 512], F32)
        sq = mk.sb([128, 8, 512], BF16)
        rs = mk.sb([128, 512], F32)
        Rt = Res()
        h2 = mk.sb([128, 8, FW], BF16)
        Rh2 = Res()
        RAW = Ring([(mk.sb([128, FW], F32), Res()) for _ in range(2)])
        ACC = Ring([(mk.sb([128, FB], F32), Res()) for _ in range(2)])
        UU = Ring([(mk.sb([128, FB], BF16), Res()) for _ in range(2)])
        aT = mk.sb([128, NFF, FB], BF16)
        RaT = Res()
        XT = Ring([(mk.sb([128, 512], F32), Res()) for _ in range(3)])
        XO = Ring([(mk.sb([128, 512], F32), Res()) for _ in range(3)])
        NBLK = SEG // FB
        for s in range(NSEG):
            for bk in range(NBLK):
                tb = s * SEG + bk * FB
                norm_h(h2, Rh2, Xmid, [RXmid], tb + 1, FW, g2, l, xp, sq, rs, Rt)
                fl_l = s if bk == 0 else None
                fl_r = s + 1 if bk == NBLK - 1 else None
                ups = [(0, 512), (512, 512), (1024, 512), (1536, 512), (2048, 512), (2560, 256)]
                for (col0, ncols) in ups:
                    wu, Rwu = wload("w_up", l, col0, ncols, 8)
                    wg, Rwg = wload("w_gate", l, col0, ncols, 8)
                    for j in range(ncols // 128):
                        f = col0 // 128 + j
                        raw, Rr = RAW.get()
                        for (c0, w) in pieces(FW):
                            p, Rp = PA.get()
                            mm_acc(p[:, 0:w], Rp, [(wu[:, kc, j * 128:(j + 1) * 128], h2[:, kc, c0:c0 + w]) for kc in range(8)], [Rwu, Rh2])
                            op("act", lambda e: e.copy(out=raw[:, c0:c0 + w], in_=p[:, 0:w]), reads=[Rp], writes=[Rr], acc=True)
                        if fl_l is not None:
                            op("dve", lambda e: e.tensor_scalar(out=raw[:, 0:1], in0=raw[:, 0:1], scalar1=flg[:, fl_l:fl_l + 1], scalar2=None, op0=ALU.mult), reads=[Rr, Rc], writes=[Rr])
                        if fl_r is not None:
                            op("dve", lambda e: e.tensor_scalar(out=raw[:, FB + 1:FB + 2], in0=raw[:, FB + 1:FB + 2], scalar1=flg[:, fl_r:fl_r + 1], scalar2=None, op0=ALU.mult), reads=[Rr, Rc], writes=[Rr])
                        ac, Rac = ACC.get()
                        op("dve", lambda e: e.tensor_scalar(out=ac[:], in0=raw[:, 0:FB], scalar1=cvF[:, l * 3, f:f + 1], scalar2=cvFb[:, l, f:f + 1], op0=ALU.mult, op1=ALU.add), reads=[Rr, Rc], writes=[Rac])
                        for k in range(1, 3):
                            op("dve", lambda e: e.scalar_tensor_tensor(out=ac[:], in0=raw[:, k:k + FB], scalar=cvF[:, l * 3 + k, f:f + 1], in1=ac[:], op0=ALU.mult, op1=ALU.add), reads=[Rr, Rc, Rac], writes=[Rac])
                        uu, Ruu = UU.get()
                        op("act", lambda e: e.activation(out=uu[:], in_=ac[:], func=AF.Silu), reads=[Rac], writes=[Ruu])
                        for (c0, w) in pieces(FB):
                            p, Rp = PA.get()
                            mm_acc(p[:, 0:w], Rp, [(wg[:, kc, j * 128:(j + 1) * 128], h2[:, kc, 1 + c0:1 + c0 + w]) for kc in range(8)], [Rwg, Rh2])
                            op("dve", lambda e: e.tensor_tensor(out=aT[:, f, c0:c0 + w], in0=p[:, 0:w], in1=uu[:, c0:c0 + w], op=ALU.mult), reads=[Rp, Ruu], writes=[RaT], acc=True)
                for c in range(8):
                    wd, Rwd = wload("w_down", l, c * 128, 128, NFF)
                    for (c0, w) in pieces(FB):
                        xt_, Rxt = XT.get()
                        dma(xt_[:], Xmid[c * 128:(c + 1) * 128, 2 + tb + c0:2 + tb + c0 + w], writes=[Rxt])
                        p, Rp = PA.get()
                        mm_acc(p[:, 0:w], Rp, [(wd[:, f, :], aT[:, f, c0:c0 + w]) for f in range(NFF)], [Rwd, RaT])
                        xo, Rxo = XO.get()
                        op("dve", lambda e: e.tensor_tensor(out=xo[:], in0=p[:, 0:w], in1=xt_[:], op=ALU.add), reads=[Rp, Rxt], writes=[Rxo])
                        if last:
                            dma(yT[c * 128:(c + 1) * 128, tb + c0:tb + c0 + w], xo[:], reads=[Rxo], q="pool")
                        else:
                            dma(XB[c * 128:(c + 1) * 128, 2 + tb + c0:2 + tb + c0 + w], xo[:], reads=[Rxo], q="pool", acc=True)
        mk.pop()

    mk.barrier()
    ninstr = mk.ninstr
    mk.close()
    return nc, ninstr


def _t5_bucket(delta):
    n = np.abs(delta)
    large = 8 + (np.log(np.maximum(n, 1).astype(np.float32) / np.float32(8)) / np.float32(np.log(1024 / 8)) * np.float32(8)).astype(np.int32)
    large = np.minimum(large, 15)
    return np.where(delta > 0, 16, 0) + np.where(n < 8, n, large)


def host_consts():
    i = np.arange(128)[:, None]
    j = np.arange(128)[None, :]
    f = lambda m: m.astype(np.float32)
    ident = f(i == j)
    U = f(i <= j)
    L = f(i >= j)
    pm_l_s = np.where(i > j, 0.0, BIG).astype(np.float32)
    pm_l_i = np.where(i >= j, 0.0, BIG).astype(np.float32)
    pm_u_s = np.where(j > i, 0.0, BIG).astype(np.float32)
    pm_u_i = np.where(j >= i, 0.0, BIG).astype(np.float32)
    ones = np.ones((128, 128), np.float32)
    bd = f((i // 64) == (j // 64)) / 64.0
    cst = np.concatenate([ident, U, L, pm_l_s, pm_l_i, pm_u_s, pm_u_i, ones, bd], axis=1)
    delta = np.arange(NF) - FOFF
    b = _t5_bucket(delta)
    oh = (b[None, :] == np.arange(32)[:, None]).astype(np.float32)
    valid = np.zeros((12, NF), np.float32)
    for g in range(3):
        d = DILS[g]
        v = ((delta % d) == 0) & (np.abs(delta) <= 64 * d)
        valid[4 * g:4 * g + 4] = v[None, :].astype(np.float32)
    return np.ascontiguousarray(cst), oh, valid


def core_inputs(x_tok, links, weights):
    NT = x_tok.shape[0]
    xTp = np.zeros((D, NT + 4), np.float32)
    xTp[:, 2:NT + 2] = x_tok.T
    cst, oh, valid = host_consts()
    m = {"xT": xTp, "flags": np.ascontiguousarray(np.broadcast_to(np.asarray(links, np.float32)[None, :], (128, len(links)))),
         "cst": cst, "oh": oh, "valid": valid}
    m.update(weights)
    return m


_CACHE = {}


def kernel(x_prompt, x_sample, ln1_g, w_in, conv_a, a_log, dt_bias, norm_a, qn_b, kn_b, rel_bias,
           w_oa, w_ob, w_out, ln2_g, w_up, w_gate, conv_ff, conv_ff_b, w_down):
    DEPTH = w_in.shape[0]
    NSEG, SEG = 5, 2048
    a32 = lambda a: np.ascontiguousarray(np.asarray(a, dtype=np.float32))
    weights = {"ln1_g": a32(ln1_g), "w_in": a32(w_in), "conv_a": a32(conv_a), "a_log": a32(a_log).reshape(DEPTH, 16),
               "dt_bias": a32(dt_bias).reshape(DEPTH, 16), "norm_a": a32(norm_a), "qn_b": a32(qn_b), "kn_b": a32(kn_b),
               "rel_bias": a32(rel_bias), "w_oa": a32(w_oa), "w_ob": a32(w_ob), "w_out": a32(w_out), "ln2_g": a32(ln2_g),
               "w_up": a32(w_up), "w_gate": a32(w_gate), "conv_ff": a32(conv_ff), "conv_ff_b": a32(conv_ff_b), "w_down": a32(w_down)}
    xp = np.asarray(x_prompt, np.float32)
    xs = np.asarray(x_sample, np.float32)
    in_maps = []
    own = []
    for c in range(8):
        if c < 2:
            toks = np.concatenate([xp[c], xs[c]], axis=0)
            links = [0, 1, 1, 1, 0, 0]
            own.append([("p", c), ("s", c)])
        else:
            ids = list(range(2 + 5 * (c - 2), 2 + 5 * (c - 1)))
            toks = np.concatenate([xs[i] for i in ids], axis=0)
            links = [0, 0, 0, 0, 0, 0]
            own.append([("s", i) for i in ids])
        in_maps.append(core_inputs(toks, links, weights))
    key = (NSEG, SEG, DEPTH)
    if key not in _CACHE:
        _CACHE[key] = build(NSEG, SEG, DEPTH)[0]
    nc = _CACHE[key]
    res = run_bass_kernel_spmd(nc, in_maps, core_ids=list(range(8)))
    y_prompt = np.empty_like(xp)
    y_sample = np.empty_like(xs)
    for c in range(8):
        y = res.results[c]["yT"].T
        pos = 0
        for kind, i in own[c]:
            n = xp.shape[1] if kind == "p" else xs.shape[1]
            (y_prompt if kind == "p" else y_sample)[i] = y[pos:pos + n]
            pos += n
    return (y_prompt, y_sample)
```

```python
import numpy as np
from contextlib import ExitStack
import concourse.bass as bass
import concourse.mybir as mybir
from concourse.bass_utils import run_bass_kernel_spmd

F32 = mybir.dt.float32
BF16 = mybir.dt.bfloat16
import os as _os
F32R = mybir.dt.float32r if _os.environ.get('KF_FP32R', '1') == '1' else mybir.dt.float32
KF_PEND = _os.environ.get('KF_PEND', '1') == '1'
KF_LOOK = int(_os.environ.get('KF_LOOK', '3'))
KF_STQ = _os.environ.get('KF_STQ', 'act')
ALU = mybir.AluOpType
AF = mybir.ActivationFunctionType

D = 1024
NIN = 8480
DFF = 2816
NFF = 22
OFF_Z, OFF_BETA, OFF_ALPHA, OFF_QB = 3072, 4096, 4112, 4128
OFF_KB, OFF_VB, OFF_GATE = 4128 + 768, 4128 + 1536, 6432
EPS = 1e-6
BIG = 30000.0
NF = 2432
FOFF = 1216
DILS = (1, 4, 16)
OFFS = (1, 2, 8)


class Res:
    __slots__ = ("w", "r")

    def __init__(self):
        self.w = {}
        self.r = {}


class PRes(Res):
    __slots__ = ()


class MK:
    def __init__(self, nc):
        self.nc = nc
        self.es = ExitStack()
        self.engs = {"pe": nc.tensor, "act": nc.scalar, "dve": nc.vector, "pool": nc.gpsimd, "sp": nc.sync}
        self.sem = {}
        for k in ["pe", "act", "dve", "pool"]:
            self.sem[k] = self.es.enter_context(nc.semaphore("s_" + k))
        self.cnt = {k: 0 for k in ["pe", "act", "dve", "pool"]}
        self.rings = {}
        for pre, n in (("d", 32), ("e", 24), ("c", 32)):
            for i in range(n):
                self.sem["%s%d" % (pre, i)] = self.es.enter_context(nc.semaphore("%s%d" % (pre, i)))
            self.rings[pre] = {"n": n, "cnt": [0] * n, "next": 0}
        self.seen = {e: {} for e in self.engs}
        self.ninstr = 0
        self.stack = [self.es]
        self.uid = 0

    def push(self):
        es = ExitStack()
        self.stack.append(es)

    def pop(self):
        self.barrier()
        self.stack.pop().close()

    def sb(self, shape, dt, name=None):
        self.uid += 1
        return self.stack[-1].enter_context(self.nc.sbuf_tensor("t%d" % self.uid, list(shape), dt))

    def ps(self, shape, dt=F32):
        self.uid += 1
        return self.stack[-1].enter_context(self.nc.psum_tensor("p%d" % self.uid, list(shape), dt))

    def need(self, eng, tok):
        key, val = tok
        if val <= 0:
            return
        if key == "pe" and eng == "pe":
            return
        s = self.seen[eng]
        if s.get(key, 0) >= val:
            return
        self.engs[eng].wait_ge(self.sem[key], val)
        self.ninstr += 1
        s[key] = val

    def _deps(self, eng, reads, writes):
        for r in reads:
            if isinstance(r, PRes):
                for t in r.w.items():
                    if t[0] != eng:
                        self.need(eng, t)
                continue
            for t in r.w.items():
                self.need(eng, t)
        for w in writes:
            if isinstance(w, PRes):
                for t in w.w.items():
                    if t[0] != eng:
                        self.need(eng, t)
                continue
            for t in w.w.items():
                self.need(eng, t)
            for t in w.r.items():
                self.need(eng, t)

    def _commit(self, key, val, reads, writes, acc):
        for r in reads:
            if isinstance(r, PRes):
                r.w[key] = val
            else:
                r.r[key] = val
        for w in writes:
            if isinstance(w, PRes):
                w.w[key] = val
            elif acc:
                w.w[key] = val
            else:
                w.w = {key: val}
                w.r = {}

    def op(self, eng, fn, reads=(), writes=(), acc=False):
        self._deps(eng, reads, writes)
        ins = fn(self.engs[eng])
        self.cnt[eng] += 1
        ins.then_inc(self.sem[eng], 1)
        self.ninstr += 1
        self._commit(eng, self.cnt[eng], reads, writes, acc)

    def dma(self, out, in_, reads=(), writes=(), q="sp", acc=False, cast=False, **kw):
        if q == "pool" and not cast:
            q = KF_STQ
        pre = "c" if cast else ("d" if q in ("sp", "act") else "e")
        assert (q == "pool") == (pre in ("c", "e"))
        rg = self.rings[pre]
        k = rg["next"]
        rg["next"] = (k + 1) % rg["n"]
        key = "%s%d" % (pre, k)
        cnts = rg["cnt"]
        self.need(q, (key, cnts[k]))
        self._deps(q, reads, writes)
        ins = self.engs[q].dma_start(out=out, in_=in_, **kw)
        cnts[k] += 16
        ins.then_inc(self.sem[key], 16)
        self.ninstr += 1
        self._commit(key, cnts[k], reads, writes, acc)

    def barrier(self):
        for eng in ["sp", "pe", "act", "dve", "pool"]:
            for k in ["pe", "act", "dve", "pool"]:
                if k != eng:
                    self.need(eng, (k, self.cnt[k]))
            for pre in ("d", "e"):
                rg = self.rings[pre]
                for i in range(rg["n"]):
                    self.need(eng, ("%s%d" % (pre, i), rg["cnt"][i]))

    def close(self):
        self.es.close()


class Ring:
    def __init__(self, items):
        self.items = items
        self.i = 0

    def get(self):
        it = self.items[self.i]
        self.i = (self.i + 1) % len(self.items)
        return it


def pieces(total, step=512):
    out = []
    c = 0
    while c < total:
        out.append((c, min(step, total - c)))
        c += step
    return out


def build(NSEG, SEG, DEPTH, FB=1024):
    NT = NSEG * SEG
    TPS = SEG // 128
    NTILE = NT // 128
    nc = bass.Bass("TRN2", target_bir_lowering=False)
    dt_in = lambda n, s: nc.dram_tensor(n, list(s), F32, kind="ExternalInput").ap()
    xT = dt_in("xT", [D, NT + 4])
    flags_d = dt_in("flags", [128, NSEG + 1])
    cst_d = dt_in("cst", [128, 9 * 128])
    oh_d = dt_in("oh", [32, NF])
    valid_d = dt_in("valid", [12, NF])
    W = {}
    for n, s in [("ln1_g", [DEPTH, D]), ("w_in", [DEPTH, D, NIN]), ("conv_a", [DEPTH, 5, 3072]),
                 ("a_log", [DEPTH, 16]), ("dt_bias", [DEPTH, 16]), ("norm_a", [DEPTH, 128]),
                 ("qn_b", [DEPTH, 64]), ("kn_b", [DEPTH, 64]), ("rel_bias", [32, 12]),
                 ("w_oa", [DEPTH, D, D]), ("w_ob", [DEPTH, 256, D]), ("w_out", [DEPTH, D, D]),
                 ("ln2_g", [DEPTH, D]), ("w_up", [DEPTH, D, DFF]), ("w_gate", [DEPTH, D, DFF]),
                 ("conv_ff", [DEPTH, 3, DFF]), ("conv_ff_b", [DEPTH, DFF]), ("w_down", [DEPTH, DFF, D])]:
        W[n] = dt_in(n, s)
    yT = nc.dram_tensor("yT", [D, NT], F32, kind="ExternalOutput").ap()

    def scr(n, s, dt=BF16):
        return nc.dram_tensor(n, list(s), dt).ap()

    WB = {n: scr("b_" + n, W[n].shape) for n in ["w_in", "w_oa", "w_ob", "w_out", "w_up", "w_gate", "w_down"]}
    XA = scr("XA", [D, NT + 4], F32)
    XB = scr("XB", [D, NT + 4], F32)
    cq = scr("cq", [8, 128, NT])
    ck = scr("ck", [8, 128, NT])
    kt = scr("kt", [NT, D])
    vt = scr("vt", [NT, D])
    zt = scr("zt", [D, NT])
    gbd = scr("gbd", [NT, 32], F32)
    aq = scr("aq", [768, NT])
    ak = scr("ak", [768, NT])
    av = scr("av", [NT, 768])
    gt = scr("gt", [2048, NT])
    ofd = scr("ofd", [NT, D], F32)
    oag = scr("oag", [D, NT])
    obd = scr("obd", [256, NT])
    Fd = scr("Fd", [12, NF], F32)

    mk = MK(nc)
    op, dma = mk.op, mk.dma

    cst = mk.sb([128, 9 * 128], F32)
    Rc = Res()
    dma(cst[:], cst_d, writes=[Rc])
    IDF, UF, LF = cst[:, 0:128], cst[:, 128:256], cst[:, 256:384]
    PM_L_S, PM_L_I, PM_U_S, PM_U_I = (cst[:, 384:512], cst[:, 512:640], cst[:, 640:768], cst[:, 768:896])
    ONESF = cst[:, 896:1024]
    cb = mk.sb([128, 3 * 128], BF16)
    op("dve", lambda e: e.tensor_copy(out=cb[:, 0:128], in_=cst[:, 0:128]), reads=[Rc], writes=[Rc])
    op("dve", lambda e: e.tensor_copy(out=cb[:, 128:256], in_=cst[:, 896:1024]), reads=[Rc], writes=[Rc])
    op("dve", lambda e: e.tensor_copy(out=cb[:, 256:384], in_=cst[:, 1024:1152]), reads=[Rc], writes=[Rc])
    IDB, ONESB, BD64B = cb[:, 0:128], cb[:, 128:256], cb[:, 256:384]
    cr = mk.sb([128, 384], F32R)
    op("dve", lambda e: e.tensor_copy(out=cr[:, 256:384], in_=cst[:, 0:128]), reads=[Rc], writes=[Rc])
    op("dve", lambda e: e.tensor_copy(out=cr[:, 0:128], in_=cst[:, 128:256]), reads=[Rc], writes=[Rc])
    op("dve", lambda e: e.tensor_copy(out=cr[:, 128:256], in_=cst[:, 256:384]), reads=[Rc], writes=[Rc])
    flg = mk.sb([128, NSEG + 1], F32)
    dma(flg[:], flags_d, writes=[Rc])

    def pload(shape, src):
        t = mk.sb(shape, F32)
        dma(t[:], src, writes=[Rc], allow_slow_non_contiguous=True)
        return t

    g1 = pload([128, DEPTH, 8], W["ln1_g"].rearrange("l (kc p) -> p l kc", p=128))
    g2 = pload([128, DEPTH, 8], W["ln2_g"].rearrange("l (kc p) -> p l kc", p=128))
    cvA = pload([128, DEPTH * 5, 24], W["conv_a"].rearrange("l k (c p) -> p (l k) c", p=128))
    cvF = pload([128, DEPTH * 3, NFF], W["conv_ff"].rearrange("l k (c p) -> p (l k) c", p=128))
    cvFb = pload([128, DEPTH, NFF], W["conv_ff_b"].rearrange("l (c p) -> p l c", p=128))
    nrmA = pload([128, DEPTH], W["norm_a"].rearrange("l p -> p l"))
    qng = mk.sb([128, DEPTH], F32)
    kng = mk.sb([128, DEPTH], F32)
    for hf in range(2):
        dma(qng[hf * 64:(hf + 1) * 64, :], W["qn_b"].rearrange("l p -> p l"), writes=[Rc], allow_slow_non_contiguous=True)
        dma(kng[hf * 64:(hf + 1) * 64, :], W["kn_b"].rearrange("l p -> p l"), writes=[Rc], allow_slow_non_contiguous=True)
    nal = mk.sb([128, DEPTH * 16], F32)
    dtb = mk.sb([128, DEPTH * 16], F32)
    dma(nal[:], bass.AP(tensor=W["a_log"].tensor, offset=0, ap=[[0, 128], [1, DEPTH * 16]]), writes=[Rc])
    dma(dtb[:], bass.AP(tensor=W["dt_bias"].tensor, offset=0, ap=[[0, 128], [1, DEPTH * 16]]), writes=[Rc])
    op("act", lambda e: e.activation(out=nal[:], in_=nal[:], func=AF.Exp), reads=[Rc], writes=[Rc])
    op("dve", lambda e: e.tensor_scalar(out=nal[:], in0=nal[:], scalar1=-1.0, scalar2=None, op0=ALU.mult), reads=[Rc], writes=[Rc])

    RW = {}

    def cast_gen(l, CSTG, CBF):
        for n in ["w_in", "w_oa", "w_ob", "w_out", "w_up", "w_gate", "w_down"]:
            K, N = W[n].shape[1], W[n].shape[2]
            RW[(n, l)] = []
            for r0 in range(0, K, 128):
                for (c0, w) in pieces(N, 2048):
                    st, Rst = CSTG.get()
                    cb_, Rcb = CBF.get()
                    dma(st[:, 0:w], W[n][l, r0:r0 + 128, c0:c0 + w], writes=[Rst])
                    op("pool", lambda e: e.tensor_copy(out=cb_[:, 0:w], in_=st[:, 0:w]), reads=[Rst], writes=[Rcb])
                    R = Res()
                    RW[(n, l)].append(R)
                    dma(WB[n][l, r0:r0 + 128, c0:c0 + w], cb_[:, 0:w], reads=[Rcb], writes=[R], q="pool")
                    yield

    mk.push()
    _g = cast_gen(0, Ring([(mk.sb([128, 2048], F32), Res()) for _ in range(3)]), Ring([(mk.sb([128, 2048], BF16), Res()) for _ in range(3)]))
    for _ in _g:
        pass
    mk.pop()

    zpad = mk.sb([128, 8, 2], F32)
    op("dve", lambda e: e.memset(zpad[:], 0.0), writes=[Rc])
    RX = {"A": Res(), "B": Res()}
    for nm, X in (("A", XA), ("B", XB)):
        for c0 in (0, NT + 2):
            dma(X[:, c0:c0 + 2].rearrange("(kc p) t -> p kc t", p=128), zpad[:], reads=[Rc])

    psA = [(mk.ps([128, 512]), PRes()) for _ in range(2)]
    psS_t = [(mk.ps([128, 512]), PRes()) for _ in range(3)]
    psT_t = [(mk.ps([128, 1024], BF16), PRes()) for _ in range(2)]
    psC_t = (mk.ps([128, 512]), PRes())
    PA = Ring(psA)
    PS = Ring([(psS_t[b][0][:, q * 128:(q + 1) * 128], psS_t[b][1]) for q in range(4) for b in range(3)])
    PT = Ring([(psT_t[b][0][:, q * 128:(q + 1) * 128], psT_t[b][1]) for q in range(4) for b in range(2)])
    PC = Ring([(psC_t[0][:, q * 128:(q + 1) * 128], psC_t[1]) for q in range(4)])

    WP = Ring([(mk.sb([128, 4096], BF16), Res()) for _ in range(3)])

    def wload(n, l, col0, ncols, nk):
        t, R = WP.get()
        v = t[:, 0:nk * ncols].rearrange("p (k c) -> p k c", k=nk)
        dma(v, WB[n][l, :, col0:col0 + ncols].rearrange("(k p) c -> p k c", p=128), reads=RW[(n, l)], writes=[R])
        return v, R

    def mm_acc(ps_ap, Rps, pairs, reads):
        def fn(e):
            n = len(pairs)
            ins = None
            for i, (l_, r_) in enumerate(pairs):
                ins = e.matmul(ps_ap, lhsT=l_, rhs=r_, start=(i == 0), stop=(i == n - 1))
            return ins
        op("pe", fn, reads=reads, writes=[Rps])

    EBIDX = {}
    for g in range(3):
        for hh in range(4):
            for off in range(-OFFS[g], OFFS[g] + 1):
                EBIDX[(g, hh, off)] = len(EBIDX)
    EB = mk.sb([128, len(EBIDX), 128], BF16)
    REB = Res()
    mk.push()
    rb = mk.sb([32, 12], F32)
    ohs = mk.sb([32, NF], F32)
    vls = mk.sb([12, NF], F32)
    fs = mk.sb([12, NF], F32)
    Rs = Res()
    dma(rb[:], W["rel_bias"], writes=[Rs])
    dma(ohs[:], oh_d, writes=[Rs])
    dma(vls[:], valid_d, writes=[Rs])
    for (c0, w) in pieces(NF):
        p, Rp = PA.get()
        mm_acc(p[0:12, 0:w], Rp, [(rb[:], ohs[:, c0:c0 + w])], [Rs])
        op("act", lambda e: e.activation(out=fs[:, c0:c0 + w], in_=p[0:12, 0:w], func=AF.Exp), reads=[Rp], writes=[Rs])
    op("dve", lambda e: e.tensor_tensor(out=fs[:], in0=fs[:], in1=vls[:], op=ALU.mult), reads=[Rs], writes=[Rs])
    RFd = Res()
    dma(Fd, fs[:], reads=[Rs], writes=[RFd])
    hk = Ring([(mk.sb([128, 128], F32), Res()) for _ in range(4)])
    for (g, hh, off), idx in EBIDX.items():
        t, R = hk.get()
        base = 128 * off + (FOFF - 127)
        src = bass.AP(tensor=Fd.tensor, offset=(4 * g + hh) * NF + base, ap=[[1, 128], [1, 128]])
        dma(t[:], src, reads=[RFd], writes=[R])
        op("dve", lambda e: e.tensor_copy(out=EB[:, idx, :], in_=t[:, ::-1]), reads=[R], writes=[REB], acc=True)
    mk.pop()

    def norm_h(hT, Rh, X, RXs, col0, ncols, gain, l, xp, sq, rs, Rt):
        for (c0, w) in pieces(ncols):
            dma(xp[:, :, 0:w], X[:, col0 + c0:col0 + c0 + w].rearrange("(kc p) t -> p kc t", p=128),
                writes=[Rt])
            op("act", lambda e: e.activation(out=sq[:, :, 0:w], in_=xp[:, :, 0:w], func=AF.Square), reads=[Rt], writes=[Rt])
            p, Rp = PA.get()
            mm_acc(p[:, 0:w], Rp, [(ONESB, sq[:, kc, 0:w]) for kc in range(8)], [Rt, Rc])
            op("act", lambda e: e.activation(out=rs[:, 0:w], in_=p[:, 0:w], func=AF.Sqrt, bias=EPS, scale=1.0 / D), reads=[Rp], writes=[Rt])
            op("dve", lambda e: e.reciprocal(out=rs[:, 0:w], in_=rs[:, 0:w]), reads=[Rt], writes=[Rt])
            for kc in range(8):
                op("dve", lambda e: e.scalar_tensor_tensor(out=hT[:, kc, c0:c0 + w], in0=xp[:, kc, 0:w], scalar=gain[:, l, kc:kc + 1],
                                                           in1=rs[:, 0:w], op0=ALU.mult, op1=ALU.mult),
                   reads=[Rt, Rc], writes=[Rh], acc=True)

    for l in range(DEPTH):
        Xin, RXin = (xT, []) if l == 0 else (XB, [RX["B"]])
        Xmid, RXmid = XA, RX["A"]
        last = (l == DEPTH - 1)

        mk.push()
        PA.items, PA.i = psA + psS_t + [psC_t], 0
        SW = SEG + 4
        xp = mk.sb([128, 8, 512], F32)
        sq = mk.sb([128, 8, 512], BF16)
        rs = mk.sb([128, 512], F32)
        Rt = Res()
        hT = mk.sb([128, 8, SW], BF16)
        Rh = Res()
        RAW = Ring([(mk.sb([128, SW], F32), Res()) for _ in range(2)])
        acc = mk.sb([128, SEG], F32)
        Racc = Res()
        sqb = mk.sb([128, SEG], BF16)
        RS2 = Ring([(mk.sb([128, 512], F32), Res()) for _ in range(2)])
        OUTB = Ring([(mk.sb([128, SW], BF16), Res()) for _ in range(3)])
        TOK = Ring([(mk.sb([128, 16, 128], BF16), Res()) for _ in range(2)])
        AVS = Ring([(mk.sb([128, 4, 512], BF16), Res()) for _ in range(2)])
        gbs = mk.sb([128, TPS, 32], F32)
        tmpg = mk.sb([128, TPS, 16], F32)
        tmpl = mk.sb([128, TPS, 16], F32)
        Rg = Res()
        Rscr1 = Res()

        for s in range(NSEG):
            t0 = s * SEG
            norm_h(hT, Rh, Xin, RXin, t0, SW, g1, l, xp, sq, rs, Rt)

            PEND = []

            def proj_fm(col0, ncols, handler, post, pcs):
                wt, Rw = wload("w_in", l, col0, ncols, 8)
                for j in range(ncols // 128):
                    st = handler(j, None, None, None, None, init=True)
                    for (c0, w) in pcs:
                        p, Rp = PA.get()
                        mm_acc(p[:, 0:w], Rp, [(wt[:, kc, j * 128:(j + 1) * 128], hT[:, kc, c0:c0 + w]) for kc in range(8)], [Rw, Rh])
                        handler(j, c0, w, p, Rp, st=st)
                    if PEND or not KF_PEND:
                        if not KF_PEND:
                            post(j, st)
                            continue
                        PEND.pop(0)()
                    PEND.append(lambda post=post, j=j, st=st: post(j, st))

            full = pieces(SW)
            inner = [(2 + c0, w) for (c0, w) in pieces(SEG)]

            for grp in range(6):
                def h_raw(j, c0, w, p, Rp, init=False, st=None):
                    if init:
                        return RAW.get()
                    raw, Rr = st
                    op("act", lambda e: e.copy(out=raw[:, c0:c0 + w], in_=p[:, 0:w]), reads=[Rp], writes=[Rr], acc=True)

                def post_a(j, st, grp=grp):
                    raw, Rr = st
                    ci = grp * 4 + j
                    op("dve", lambda e: e.tensor_scalar(out=raw[:, 0:2], in0=raw[:, 0:2], scalar1=flg[:, s:s + 1], scalar2=None, op0=ALU.mult), reads=[Rr, Rc], writes=[Rr])
                    op("dve", lambda e: e.tensor_scalar(out=raw[:, SEG + 2:SEG + 4], in0=raw[:, SEG + 2:SEG + 4], scalar1=flg[:, s + 1:s + 2], scalar2=None, op0=ALU.mult), reads=[Rr, Rc], writes=[Rr])
                    op("act", lambda e: e.activation(out=acc[:], in_=raw[:, 0:SEG], func=AF.Copy, scale=cvA[:, l * 5, ci:ci + 1]), reads=[Rr, Rc], writes=[Racc])
                    for k in range(1, 5):
                        op("dve", lambda e: e.scalar_tensor_tensor(out=acc[:], in0=raw[:, k:k + SEG], scalar=cvA[:, l * 5 + k, ci:ci + 1], in1=acc[:], op0=ALU.mult, op1=ALU.add), reads=[Rr, Rc, Racc], writes=[Racc])
                    ob, Ro = OUTB.get()
                    if ci < 16:
                        op("act", lambda e: e.activation(out=acc[:], in_=acc[:], func=AF.Silu), reads=[Racc], writes=[Racc])
                        op("act", lambda e: e.activation(out=sqb[:], in_=acc[:], func=AF.Square), reads=[Racc], writes=[Racc])
                        for (c0, w) in pieces(SEG):
                            p, Rp = PA.get()
                            mm_acc(p[:, 0:w], Rp, [(ONESB, sqb[:, c0:c0 + w])], [Racc, Rc])
                            r2, Rr2 = RS2.get()
                            op("act", lambda e: e.activation(out=r2[:, 0:w], in_=p[:, 0:w], func=AF.Sqrt, bias=EPS, scale=1.0), reads=[Rp], writes=[Rr2])
                            op("dve", lambda e: e.reciprocal(out=r2[:, 0:w], in_=r2[:, 0:w]), reads=[Rr2], writes=[Rr2])
                            sc = (128.0 ** -0.5) if ci < 8 else 1.0
                            op("dve", lambda e: e.scalar_tensor_tensor(out=ob[:, c0:c0 + w], in0=acc[:, c0:c0 + w], scalar=sc, in1=r2[:, 0:w], op0=ALU.mult, op1=ALU.mult), reads=[Racc, Rr2], writes=[Ro], acc=True)
                        dst = cq if ci < 8 else ck
                        dma(dst[ci % 8, :, t0:t0 + SEG], ob[:, 0:SEG], reads=[Ro], q="pool", acc=True)
                    else:
                        op("act", lambda e: e.activation(out=ob[:, 0:SEG], in_=acc[:], func=AF.Silu), reads=[Racc], writes=[Ro])
                    if ci >= 8:
                        tk, Rtk = TOK.get()
                        for jt in range(TPS):
                            pt, Rpt = PT.get()
                            op("pe", lambda e: e.transpose(out=pt, in_=ob[:, jt * 128:(jt + 1) * 128], identity=IDB), reads=[Ro, Rc], writes=[Rpt])
                            eng = "act" if jt % 2 == 0 else "dve"
                            if eng == "act":
                                op("act", lambda e: e.copy(out=tk[:, jt, :], in_=pt), reads=[Rpt], writes=[Rtk], acc=True)
                            else:
                                op("dve", lambda e: e.tensor_copy(out=tk[:, jt, :], in_=pt), reads=[Rpt], writes=[Rtk], acc=True)
                        dst = kt if ci < 16 else vt
                        hcol = (ci % 8) * 128
                        dma(dst[t0:t0 + SEG, hcol:hcol + 128].rearrange("(j p) d -> p j d", p=128), tk[:, 0:TPS, :], reads=[Rtk], q="pool", acc=True)

                proj_fm(grp * 512, 512, h_raw, post_a, full)

            def act_group(col0, func, dst, row0):
                def h(j, c0, w, p, Rp, init=False, st=None):
                    if init:
                        return OUTB.get()
                    ob, Ro = st
                    op("act", lambda e: e.activation(out=ob[:, c0:c0 + w], in_=p[:, 0:w], func=func), reads=[Rp], writes=[Ro], acc=True)

                def post(j, st):
                    ob, Ro = st
                    dma(dst[row0 + j * 128:row0 + (j + 1) * 128, t0:t0 + SEG], ob[:, 2:2 + SEG], reads=[Ro], q="pool", acc=True)
                proj_fm(col0, 512, h, post, inner)

            for grp in range(2):
                act_group(OFF_Z + grp * 512, AF.Silu, zt, grp * 512)
            for grp in range(4):
                act_group(OFF_GATE + grp * 512, AF.Sigmoid, gt, grp * 512)

            for (col0, dst, gn) in ((OFF_QB, aq, qng), (OFF_KB, ak, kng)):
                for (cc, ncols) in ((0, 512), (512, 256)):
                    def h_raw2(j, c0, w, p, Rp, init=False, st=None):
                        if init:
                            return RAW.get()
                        raw, Rr = st
                        op("act", lambda e: e.copy(out=raw[:, c0:c0 + w], in_=p[:, 0:w]), reads=[Rp], writes=[Rr], acc=True)

                    def post_b(j, st, cc=cc, dst=dst, gn=gn):
                        raw, Rr = st
                        ob, Ro = OUTB.get()
                        op("act", lambda e: e.activation(out=sqb[:], in_=raw[:, 2:2 + SEG], func=AF.Square), reads=[Rr], writes=[Racc])
                        for (c0, w) in pieces(SEG):
                            p, Rp = PA.get()
                            mm_acc(p[:, 0:w], Rp, [(BD64B, sqb[:, c0:c0 + w])], [Racc, Rc])
                            r2, Rr2 = RS2.get()
                            op("act", lambda e: e.activation(out=r2[:, 0:w], in_=p[:, 0:w], func=AF.Sqrt, bias=EPS, scale=1.0), reads=[Rp], writes=[Rr2])
                            op("dve", lambda e: e.reciprocal(out=r2[:, 0:w], in_=r2[:, 0:w]), reads=[Rr2], writes=[Rr2])
                            op("dve", lambda e: e.scalar_tensor_tensor(out=ob[:, c0:c0 + w], in0=raw[:, 2 + c0:2 + c0 + w], scalar=gn[:, l:l + 1], in1=r2[:, 0:w], op0=ALU.mult, op1=ALU.mult), reads=[Rr, Rr2, Rc], writes=[Ro], acc=True)
                        r0 = cc + j * 128
                        dma(dst[r0:r0 + 128, t0:t0 + SEG], ob[:, 0:SEG], reads=[Ro], q="pool", acc=True)
                    proj_fm(col0 + cc, ncols, h_raw2, post_b, inner)

            while PEND:
                PEND.pop(0)()
            for (cc, ncols) in ((0, 512), (512, 256)):
                wt, Rw = wload("w_in", l, OFF_VB + cc, ncols, 8)
                for q4 in range(TPS // 4):
                    sv, Rsv = AVS.get()
                    for jj in range(4):
                        jt = q4 * 4 + jj
                        p, Rp = PA.get()
                        mm_acc(p[:, 0:ncols], Rp, [(hT[:, kc, 2 + jt * 128:2 + (jt + 1) * 128], wt[:, kc, :]) for kc in range(8)], [Rw, Rh])
                        if jj % 2 == 0:
                            op("act", lambda e: e.copy(out=sv[:, jj, 0:ncols], in_=p[:, 0:ncols]), reads=[Rp], writes=[Rsv], acc=True)
                        else:
                            op("dve", lambda e: e.tensor_copy(out=sv[:, jj, 0:ncols], in_=p[:, 0:ncols]), reads=[Rp], writes=[Rsv], acc=True)
                    r0 = t0 + q4 * 512
                    dma(av[r0:r0 + 512, cc:cc + ncols].rearrange("(j p) d -> p j d", p=128), sv[:, :, 0:ncols], reads=[Rsv], q="pool", acc=True)
            wt, Rw = wload("w_in", l, OFF_BETA, 32, 8)
            p, Rp = PA.get()
            for jt in range(TPS):
                mm_acc(p[:, jt * 32:(jt + 1) * 32], Rp, [(hT[:, kc, 2 + jt * 128:2 + (jt + 1) * 128], wt[:, kc, :]) for kc in range(8)], [Rw, Rh])
            pv = p[:, 0:TPS * 32].rearrange("p (j c) -> p j c", c=32)
            op("act", lambda e: e.activation(out=gbs[:, :, 16:32], in_=pv[:, :, 0:16], func=AF.Sigmoid), reads=[Rp], writes=[Rg])
            for jt in range(TPS):
                op("dve", lambda e: e.tensor_tensor(out=tmpg[:, jt, :], in0=pv[:, jt, 16:32], in1=dtb[:, l * 16:(l + 1) * 16], op=ALU.add), reads=[Rp, Rc, Rg], writes=[Rg])
            op("act", lambda e: e.activation(out=tmpl[:], in_=tmpg[:], func=AF.Abs), reads=[Rg], writes=[Rg])
            op("act", lambda e: e.activation(out=tmpl[:], in_=tmpl[:], func=AF.Exp, scale=-1.0), reads=[Rg], writes=[Rg])
            op("act", lambda e: e.activation(out=tmpl[:], in_=tmpl[:], func=AF.Ln, bias=1.0), reads=[Rg], writes=[Rg])
            op("dve", lambda e: e.scalar_tensor_tensor(out=tmpg[:], in0=tmpg[:], scalar=0.0, in1=tmpl[:], op0=ALU.max, op1=ALU.add), reads=[Rg], writes=[Rg])
            for jt in range(TPS):
                op("dve", lambda e: e.tensor_tensor(out=gbs[:, jt, 0:16], in0=tmpg[:, jt, :], in1=nal[:, l * 16:(l + 1) * 16], op=ALU.mult), reads=[Rg, Rc], writes=[Rg])
            dma(gbd[t0:t0 + SEG, :].rearrange("(j p) c -> p j c", p=128), gbs[:], reads=[Rg], q="pool", acc=True)
        mk.pop()

        PA.items, PA.i = psA, 0
        mk.push()
        NB = 2
        qT8 = [(mk.sb([128, 8, 128], BF16), Res()) for _ in range(NB)]
        kT8 = [(mk.sb([128, 8, 128], BF16), Res()) for _ in range(NB)]
        ktk = [(mk.sb([128, D], BF16), Res()) for _ in range(NB)]
        vtk = [(mk.sb([128, D], BF16), Res()) for _ in range(NB)]
        gbt = [(mk.sb([128, 32], F32), Res()) for _ in range(NB)]
        oft = [(mk.sb([128, D], F32), Res()) for _ in range(NB)]
        zs8 = [(mk.sb([128, 8, 128], BF16), Res()) for _ in range(NB)]
        sm = [(mk.sb([128, 6, 16], F32), Res()) for _ in range(NB)]
        gr_ = [mk.sb([128, 8], F32R) for _ in range(NB)]
        named = [[{k: (mk.sb([128, 128], BF16), Res()) for k in ("MT", "R", "bv", "kd")} for h in range(8)] for _ in range(NB)]
        TB = Ring([(mk.sb([128, 128], BF16), Res()) for _ in range(40)])
        TF = Ring([(mk.sb([128, 128], F32), Res()) for _ in range(24)])
        TR = Ring([(mk.sb([128, 128], F32R), Res()) for _ in range(40)])
        Sst = [(mk.sb([128, 128], F32), Res()) for _ in range(8)]
        Sbf = [(mk.sb([128, 128], BF16), Res()) for _ in range(8)]
        ostg = [(mk.sb([128, 8, 128], F32), Res()) for _ in range(2)]
        ogst = [(mk.sb([128, 8, 128], BF16), Res()) for _ in range(2)]
        ssq = [(mk.sb([128, 16], F32), Res()) for _ in range(2)]
        junk = mk.sb([128, 128], BF16)
        Rj = Res()
        Rof = Res()
        Roag = Res()

        for dr in range(2):
            chunk_order = list(range(NTILE)) if dr == 0 else list(range(NTILE - 1, -1, -1))
            pmA = PM_L_S if dr == 0 else PM_U_S
            pmT = PM_U_I if dr == 0 else PM_L_I
            CUM = UF if dr == 0 else LF
            CUMR = cr[:, 0:128] if dr == 0 else cr[:, 128:256]

            def load_chunk(ci_, b):
                c = chunk_order[ci_]
                tk0 = c * 128
                dma(qT8[b][0][:], cq[:, :, tk0:tk0 + 128].rearrange("h d t -> d h t"), writes=[qT8[b][1]])
                dma(kT8[b][0][:], ck[:, :, tk0:tk0 + 128].rearrange("h d t -> d h t"), writes=[kT8[b][1]])
                dma(ktk[b][0][:], kt[tk0:tk0 + 128, :], writes=[ktk[b][1]])
                dma(vtk[b][0][:], vt[tk0:tk0 + 128, :], writes=[vtk[b][1]])
                dma(gbt[b][0][:], gbd[tk0:tk0 + 128, :], writes=[gbt[b][1]])
                if dr == 1:
                    dma(oft[b][0][:], ofd[tk0:tk0 + 128, :], writes=[oft[b][1]])
                    dma(zs8[b][0][:], zt[:, tk0:tk0 + 128].rearrange("(h d) t -> d h t", d=128), writes=[zs8[b][1]])

            load_chunk(0, 0)
            for ci_ in range(NTILE):
                b = ci_ % NB
                c = chunk_order[ci_]
                s = c // TPS
                if ci_ + 1 < NTILE:
                    load_chunk(ci_ + 1, (ci_ + 1) % NB)
                q8, Rq8 = qT8[b]
                k8, Rk8 = kT8[b]
                kk_, Rkk = ktk[b]
                vv_, Rvv = vtk[b]
                gb_, Rgb = gbt[b]
                smt, Rsm = sm[b]
                GC, EG, EKD, NBE, EGL, TMP = (smt[:, i, :] for i in range(6))
                g0 = dr * 8
                grr = gr_[b]
                op("dve", lambda e: e.tensor_copy(out=grr[:], in_=gb_[:, g0:g0 + 8]), reads=[Rgb, Rsm], writes=[Rsm])
                p, Rp = PS.get()
                op("pe", lambda e: e.matmul(p[:, 0:8], lhsT=CUM, rhs=gb_[:, g0:g0 + 8], start=True, stop=True), reads=[Rgb, Rc], writes=[Rp])
                op("pe", lambda e: e.matmul(p[:, 8:16], lhsT=ONESF, rhs=gb_[:, g0:g0 + 8], start=True, stop=True), reads=[Rgb, Rc], writes=[Rp], acc=True)
                op("act", lambda e: e.copy(out=GC[:, 0:8], in_=p[:, 0:8]), reads=[Rp], writes=[Rsm])
                op("act", lambda e: e.activation(out=EG[:, 0:8], in_=p[:, 0:8], func=AF.Exp), reads=[Rp], writes=[Rsm], acc=True)
                op("act", lambda e: e.activation(out=EGL[:, 0:8], in_=p[:, 8:16], func=AF.Exp), reads=[Rp], writes=[Rsm], acc=True)
                op("dve", lambda e: e.tensor_tensor(out=TMP[:, 0:8], in0=p[:, 8:16], in1=GC[:, 0:8], op=ALU.subtract), reads=[Rp, Rsm], writes=[Rsm])
                op("act", lambda e: e.activation(out=EKD[:, 0:8], in_=TMP[:, 0:8], func=AF.Exp), reads=[Rsm], writes=[Rsm])
                op("dve", lambda e: e.scalar_tensor_tensor(out=NBE[:, 0:8], in0=gb_[:, 16 + g0:24 + g0], scalar=-1.0, in1=EG[:, 0:8], op0=ALU.mult, op1=ALU.mult), reads=[Rgb, Rsm], writes=[Rsm])
                first_in_seg = (c % TPS == 0) if dr == 0 else (c % TPS == TPS - 1)
                if first_in_seg:
                    fcol = s if dr == 0 else s + 1
                    for h in range(8):
                        if ci_ == 0:
                            op("dve", lambda e: e.memset(Sst[h][0][:], 0.0), writes=[Sst[h][1]])
                            op("pool", lambda e: e.memset(Sbf[h][0][:], 0.0), writes=[Sbf[h][1]])
                        else:
                            op("dve", lambda e: e.tensor_scalar(out=Sst[h][0][:], in0=Sst[h][0][:], scalar1=flg[:, fcol:fcol + 1], scalar2=None, op0=ALU.mult), reads=[Sst[h][1], Rc], writes=[Sst[h][1]])
                            op("pool", lambda e: e.tensor_scalar(out=Sbf[h][0][:], in0=Sbf[h][0][:], scalar1=flg[:, fcol:fcol + 1], scalar2=None, op0=ALU.mult), reads=[Sbf[h][1], Rc], writes=[Sbf[h][1]])
                for hg in range(2):
                    HS = list(range(hg * 4, hg * 4 + 4))
                    beta = {h: gb_[:, 16 + g0 + h:17 + g0 + h] for h in HS}
                    pkk, pqk, pgr, t1, t2, dec, decT, A_, B_, ptb, Rm = ({} for _ in range(11))
                    for h in HS:
                        kTh = k8[:, h, :]
                        pkk[h] = PS.get()
                        mm_acc(pkk[h][0], pkk[h][1], [(kTh, kTh)], [Rk8])
                        pqk[h] = PS.get()
                        mm_acc(pqk[h][0], pqk[h][1], [(kTh, q8[:, h, :])], [Rk8, Rq8])
                        pgr[h] = PS.get()
                        mm_acc(pgr[h][0], pgr[h][1], [(grr[:, h:h + 1].to_broadcast([128, 128]), CUMR)], [Rsm, Rc])
                    for h in HS:
                        t1[h] = TF.get()
                        t2[h] = TF.get()
                        op("dve", lambda e: e.scalar_tensor_tensor(out=t1[h][0][:], in0=pgr[h][0], scalar=GC[:, h:h + 1], in1=pmA, op0=ALU.subtract, op1=ALU.add), reads=[pgr[h][1], Rsm, Rc], writes=[t1[h][1]])
                        op("dve", lambda e: e.scalar_tensor_tensor(out=t2[h][0][:], in0=pgr[h][0], scalar=GC[:, h:h + 1], in1=pmT, op0=ALU.subtract, op1=ALU.subtract), reads=[pgr[h][1], Rsm, Rc], writes=[t2[h][1]])
                    for h in HS:
                        dec[h] = TF.get()
                        decT[h] = TF.get()
                        op("act", lambda e: e.activation(out=dec[h][0][:], in_=t1[h][0][:], func=AF.Exp, scale=-1.0), reads=[t1[h][1]], writes=[dec[h][1]])
                        op("act", lambda e: e.activation(out=decT[h][0][:], in_=t2[h][0][:], func=AF.Exp), reads=[t2[h][1]], writes=[decT[h][1]])
                    for h in HS:
                        A_[h] = TR.get()
                        op("dve", lambda e: e.scalar_tensor_tensor(out=A_[h][0][:], in0=pkk[h][0], scalar=beta[h], in1=dec[h][0][:], op0=ALU.mult, op1=ALU.mult), reads=[pkk[h][1], Rgb, dec[h][1]], writes=[A_[h][1]])
                        MT, RMT = named[b][h]["MT"]
                        op("dve", lambda e: e.tensor_tensor(out=MT[:], in0=pqk[h][0], in1=decT[h][0][:], op=ALU.mult), reads=[pqk[h][1], decT[h][1]], writes=[RMT])
                    for h in HS:
                        ptb[h] = PS.get()
                        op("pe", lambda e: e.matmul(ptb[h][0], lhsT=A_[h][0][:], rhs=cr[:, 256:384], start=True, stop=True), reads=[A_[h][1], Rc], writes=[ptb[h][1]])
                    for h in HS:
                        B_[h] = TR.get()
                        op("act", lambda e: e.copy(out=B_[h][0][:], in_=ptb[h][0]), reads=[ptb[h][1]], writes=[B_[h][1]])
                    for h in HS:
                        Rm[h] = TR.get()
                        op("pool", lambda e: e.tensor_tensor(out=Rm[h][0][:], in0=IDF, in1=B_[h][0][:], op=ALU.subtract), reads=[B_[h][1], Rc], writes=[Rm[h][1]])
                    P_ = dict(A_)
                    Q_ = dict(B_)
                    for k in range(1, 7):
                        pp, pq, pr, Pn, Qn, Rn = ({} for _ in range(6))
                        for h in HS:
                            pp[h] = PS.get()
                            mm_acc(pp[h][0], pp[h][1], [(Q_[h][0][:], P_[h][0][:])], [Q_[h][1], P_[h][1]])
                            if k < 6:
                                pq[h] = PS.get()
                                mm_acc(pq[h][0], pq[h][1], [(P_[h][0][:], Q_[h][0][:])], [Q_[h][1], P_[h][1]])
                        for h in HS:
                            Pn[h] = TR.get()
                            op("act", lambda e: e.copy(out=Pn[h][0][:], in_=pp[h][0]), reads=[pp[h][1]], writes=[Pn[h][1]])
                            if k < 6:
                                Qn[h] = TR.get()
                                op("dve", lambda e: e.tensor_copy(out=Qn[h][0][:], in_=pq[h][0]), reads=[pq[h][1]], writes=[Qn[h][1]])
                        for h in HS:
                            pr[h] = PS.get()
                            mm_acc(pr[h][0], pr[h][1], [(Pn[h][0][:], Rm[h][0][:])], [Pn[h][1], Rm[h][1]])
                        for h in HS:
                            Rn[h] = TR.get() if k < 6 else named[b][h]["R"]
                            op("dve", lambda e: e.tensor_tensor(out=Rn[h][0][:], in0=pr[h][0], in1=Rm[h][0][:], op=ALU.add), reads=[pr[h][1], Rm[h][1]], writes=[Rn[h][1]])
                        Rm = Rn
                        P_ = Pn
                        if k < 6:
                            Q_ = Qn
                    for h in HS:
                        bv, Rbv = named[b][h]["bv"]
                        op("act", lambda e: e.activation(out=bv[:], in_=vv_[:, h * 128:(h + 1) * 128], func=AF.Copy, scale=beta[h]), reads=[Rvv, Rgb], writes=[Rbv])
                        kd, Rkd = named[b][h]["kd"]
                        op("act", lambda e: e.activation(out=kd[:], in_=kk_[:, h * 128:(h + 1) * 128], func=AF.Copy, scale=EKD[:, h:h + 1]), reads=[Rkk, Rsm], writes=[Rkd])
                ost, Rost = ostg[ci_ % 2]
                pks, pvn, pqs, pmv, vnw, rr = {}, {}, {}, {}, {}, {}
                for h in range(8):
                    pks[h] = PS.get()
                    mm_acc(pks[h][0], pks[h][1], [(k8[:, h, :], Sbf[h][0][:])], [Rk8, Sbf[h][1]])
                for h in range(8):
                    rr[h] = TB.get()
                    op("dve", lambda e: e.scalar_tensor_tensor(out=rr[h][0][:], in0=pks[h][0], scalar=NBE[:, h:h + 1], in1=named[b][h]["bv"][0][:], op0=ALU.mult, op1=ALU.add),
                       reads=[pks[h][1], Rsm, named[b][h]["bv"][1]], writes=[rr[h][1]])
                for h in range(8):
                    pvn[h] = PS.get()
                    mm_acc(pvn[h][0], pvn[h][1], [(named[b][h]["R"][0][:], rr[h][0][:])], [named[b][h]["R"][1], rr[h][1]])
                for h in range(8):
                    vnw[h] = TB.get()
                    op("act", lambda e: e.copy(out=vnw[h][0][:], in_=pvn[h][0]), reads=[pvn[h][1]], writes=[vnw[h][1]])
                for h in range(8):
                    pqs[h] = PS.get()
                    mm_acc(pqs[h][0], pqs[h][1], [(q8[:, h, :], Sbf[h][0][:])], [Rq8, Sbf[h][1]])
                    tq, Rtq = TF.get()
                    op("act", lambda e: e.activation(out=tq[:], in_=pqs[h][0], func=AF.Copy, scale=EG[:, h:h + 1]), reads=[pqs[h][1], Rsm], writes=[Rtq])
                    pmv[h] = PS.get()
                    mm_acc(pmv[h][0], pmv[h][1], [(named[b][h]["MT"][0][:], vnw[h][0][:])], [named[b][h]["MT"][1], vnw[h][1]])
                    op("dve", lambda e: e.tensor_tensor(out=ost[:, h, :], in0=pmv[h][0], in1=tq[:], op=ALU.add), reads=[pmv[h][1], Rtq], writes=[Rost], acc=(h > 0))
                    psu, Rpsu = PS.get()
                    mm_acc(psu, Rpsu, [(named[b][h]["kd"][0][:], vnw[h][0][:])], [named[b][h]["kd"][1], vnw[h][1]])
                    op("dve", lambda e: e.scalar_tensor_tensor(out=Sst[h][0][:], in0=Sst[h][0][:], scalar=EGL[:, h:h + 1], in1=psu, op0=ALU.mult, op1=ALU.add), reads=[Rpsu, Rsm, Sst[h][1]], writes=[Sst[h][1]])
                    op("act", lambda e: e.copy(out=Sbf[h][0][:], in_=Sst[h][0][:]), reads=[Sst[h][1]], writes=[Sbf[h][1]])
                tk0 = c * 128
                if dr == 0:
                    dma(ofd[tk0:tk0 + 128, :], ost[:].rearrange("p h d -> p (h d)"), reads=[Rost], q="pool", acc=True)
                else:
                    of_, Rof_ = oft[b]
                    z8, Rz8 = zs8[b]
                    ogs, Rogs = ogst[ci_ % 2]
                    sq_, Rsq = ssq[ci_ % 2]
                    op("pool", lambda e: e.tensor_tensor(out=ost[:].rearrange("p h d -> p (h d)"), in0=ost[:].rearrange("p h d -> p (h d)"), in1=of_[:], op=ALU.add), reads=[Rost, Rof_], writes=[Rost])
                    op("dve", lambda e: e.memset(sq_[:], 0.0), writes=[Rsq])
                    for h in range(8):
                        op("act", lambda e: e.activation(out=junk[:], in_=ost[:, h, :], func=AF.Square, accum_out=sq_[:, h:h + 1]), reads=[Rost, Rsq], writes=[Rj, Rsq])
                    op("act", lambda e: e.activation(out=sq_[:, 8:16], in_=sq_[:, 0:8], func=AF.Sqrt, bias=EPS, scale=1.0 / 128), reads=[Rsq], writes=[Rsq])
                    op("dve", lambda e: e.reciprocal(out=sq_[:, 8:16], in_=sq_[:, 8:16]), reads=[Rsq], writes=[Rsq])
                    for h in range(8):
                        on, Ron = TB.get()
                        op("act", lambda e: e.activation(out=on[:], in_=ost[:, h, :], func=AF.Copy, scale=sq_[:, 8 + h:9 + h]), reads=[Rost, Rsq], writes=[Ron])
                        pt, Rpt = PT.get()
                        op("pe", lambda e: e.transpose(out=pt, in_=on[:], identity=IDB), reads=[Ron, Rc], writes=[Rpt])
                        op("dve", lambda e: e.scalar_tensor_tensor(out=ogs[:, h, :], in0=pt, scalar=nrmA[:, l:l + 1], in1=z8[:, h, :], op0=ALU.mult, op1=ALU.mult), reads=[Rpt, Rc, Rz8], writes=[Rogs], acc=(h > 0))
                    dma(oag[:, tk0:tk0 + 128].rearrange("(h d) t -> d h t", d=128), ogs[:], reads=[Rogs], q="pool", acc=True)
            mk.barrier()
        mk.pop()

        mk.push()
        WIN = 1024
        KW = SEG + 2 * WIN
        kTw = [(mk.sb([64, KW], BF16), Res()) for _ in range(3)]
        qTw = [(mk.sb([64, SEG], BF16), Res()) for _ in range(3)]
        vaw = [(mk.sb([128, KW // 128, 65], BF16), Res()) for _ in range(3)]
        for g in range(3):
            op("dve", lambda e: e.memset(vaw[g][0][:], 1.0), writes=[vaw[g][1]])
        ET = Ring([(mk.sb([128, 128], BF16), Res()) for _ in range(8)])
        PTl = Ring([(mk.sb([128, 128], BF16), Res()) for _ in range(8)])
        PC = Ring([(psC_t[0][:, 0:128], psC_t[1]), (psA[0][0][:, 0:128], psA[0][1]), (psA[1][0][:, 0:128], psA[1][1])])
        obuf = mk.sb([128, TPS, 256], BF16)
        Rob = Res()
        obT = [(mk.sb([128, 2, SEG], BF16), Res()) for _ in range(2)]
        rden = Ring([(mk.sb([128, 1], F32), Res()) for _ in range(4)])
        Robd = Res()
        cgen = None
        if l + 1 < DEPTH:
            cgen = cast_gen(l + 1, Ring([(mk.sb([128, 2048], F32), Res()) for _ in range(2)]), Ring([(mk.sb([128, 2048], BF16), Res()) for _ in range(2)]))
        for s in range(NSEG):
            t0 = s * SEG
            lo = max(0, t0 - WIN)
            hi = min(NT, t0 + SEG + WIN)
            w0 = lo - (t0 - WIN)
            for hh in range(4):
                for g in range(3):
                    hd = 4 * g + hh
                    dma(kTw[g][0][:, w0:w0 + hi - lo], ak[hd * 64:(hd + 1) * 64, lo:hi], writes=[kTw[g][1]])
                    dma(qTw[g][0][:], aq[hd * 64:(hd + 1) * 64, t0:t0 + SEG], writes=[qTw[g][1]])
                    dma(vaw[g][0][:, w0 // 128:(w0 + hi - lo) // 128, 0:64], av[lo:hi, hd * 64:(hd + 1) * 64].rearrange("(j p) d -> p j d", p=128), writes=[vaw[g][1]])
                if cgen is not None:
                    for _ in range(-(-120 // (NSEG * 4))):
                        next(cgen, None)
                steps_all = []
                for qt in range(TPS):
                    qabs = s * TPS + qt
                    steps = []
                    for g in range(3):
                        for off in range(-OFFS[g], OFFS[g] + 1):
                            kabs = qabs + off
                            if kabs < 0 or kabs >= NTILE:
                                continue
                            sk = kabs // TPS
                            fl = None if sk == s else (s if sk < s else s + 1)
                            steps.append((g, off, kabs, fl))
                    for i, st_ in enumerate(steps):
                        steps_all.append((qt, i, len(steps)) + st_)
                LOOK = KF_LOOK
                ptiles = {}
                cur = {}
                for idx in range(len(steps_all) + LOOK):
                    if idx < len(steps_all):
                        qt, i, n, g, off, kabs, fl = steps_all[idx]
                        kc0 = kabs * 128 - (t0 - WIN)
                        pS, RpS = PS.get()
                        mm_acc(pS, RpS, [(kTw[g][0][:, kc0:kc0 + 128], qTw[g][0][:, qt * 128:(qt + 1) * 128])], [kTw[g][1], qTw[g][1]])
                        E_, RE = ET.get()
                        op("act", lambda e: e.activation(out=E_[:], in_=pS, func=AF.Exp, scale=0.125), reads=[RpS], writes=[RE])
                        P_, RP = PTl.get()
                        ebt = EB[:, EBIDX[(g, hh, off)], :]
                        if fl is None:
                            op("dve", lambda e: e.tensor_tensor(out=P_[:], in0=E_[:], in1=ebt, op=ALU.mult), reads=[RE, REB], writes=[RP])
                        else:
                            op("dve", lambda e: e.scalar_tensor_tensor(out=P_[:], in0=E_[:], scalar=flg[:, fl:fl + 1], in1=ebt, op0=ALU.mult, op1=ALU.mult), reads=[RE, REB, Rc], writes=[RP])
                        ptiles[idx] = (P_, RP)
                    if idx >= LOOK:
                        qt, i, n, g, off, kabs, fl = steps_all[idx - LOOK]
                        P_, RP = ptiles.pop(idx - LOOK)
                        kc0 = kabs * 128 - (t0 - WIN)
                        if i == 0:
                            cur["pc"] = PC.get()
                        pc, Rpc = cur["pc"]
                        va = vaw[g][0][:, kc0 // 128, :]
                        op("pe", lambda e: e.matmul(pc[:, 0:65], lhsT=P_[:], rhs=va, start=(i == 0), stop=(i == n - 1)), reads=[RP, vaw[g][1]], writes=[Rpc], acc=(i > 0))
                        if i == n - 1:
                            rd, Rrd = rden.get()
                            op("dve", lambda e: e.reciprocal(out=rd[:], in_=pc[:, 64:65]), reads=[Rpc], writes=[Rrd])
                            op("act", lambda e: e.activation(out=obuf[:, qt, hh * 64:(hh + 1) * 64], in_=pc[:, 0:64], func=AF.Copy, scale=rd[:, 0:1]), reads=[Rpc, Rrd], writes=[Rob], acc=True)
            oT, RoT = obT[s % 2]
            for qt in range(TPS):
                for hf in range(2):
                    pt, Rpt = PT.get()
                    op("pe", lambda e: e.transpose(out=pt, in_=obuf[:, qt, hf * 128:(hf + 1) * 128], identity=IDB), reads=[Rob, Rc], writes=[Rpt])
                    if hf == 0:
                        op("act", lambda e: e.copy(out=oT[:, hf, qt * 128:(qt + 1) * 128], in_=pt), reads=[Rpt], writes=[RoT], acc=True)
                    else:
                        op("dve", lambda e: e.tensor_copy(out=oT[:, hf, qt * 128:(qt + 1) * 128], in_=pt), reads=[Rpt], writes=[RoT], acc=True)
            dma(obd[:, t0:t0 + SEG].rearrange("(k p) t -> p k t", p=128), oT[:], reads=[RoT], q="pool", acc=True)
        if cgen is not None:
            for _ in cgen:
                pass
        mk.pop()

        mk.push()
        PA.items, PA.i = psA + psS_t + [psC_t], 0
        oaT = mk.sb([128, 8, SEG], BF16)
        obT5 = mk.sb([128, 2, SEG], BF16)
        Rin5 = Res()
        mixed = mk.sb([128, 8, SEG], BF16)
        Rmx = Res()
        GT = Ring([(mk.sb([128, 2, 512], BF16), Res()) for _ in range(3)])
        M1 = Ring([(mk.sb([128, 512], F32), Res()) for _ in range(2)])
        M2 = Ring([(mk.sb([128, 512], F32), Res()) for _ in range(2)])
        XT = Ring([(mk.sb([128, 512], F32), Res()) for _ in range(3)])
        XO = Ring([(mk.sb([128, 512], F32), Res()) for _ in range(3)])
        for s in range(NSEG):
            t0 = s * SEG
            dma(oaT[:], oag[:, t0:t0 + SEG].rearrange("(k p) t -> p k t", p=128), writes=[Rin5])
            dma(obT5[:], obd[:, t0:t0 + SEG].rearrange("(k p) t -> p k t", p=128), writes=[Rin5], acc=True)
            for grp in range(2):
                wa, Rwa = wload("w_oa", l, grp * 512, 512, 8)
                wb_, Rwb = wload("w_ob", l, grp * 512, 512, 2)
                for j in range(4):
                    c = grp * 4 + j
                    for (c0, w) in pieces(SEG):
                        gtile, Rgt = GT.get()
                        dma(gtile[:, 0, :], gt[c * 128:(c + 1) * 128, t0 + c0:t0 + c0 + w], writes=[Rgt])
                        dma(gtile[:, 1, :], gt[1024 + c * 128:1024 + (c + 1) * 128, t0 + c0:t0 + c0 + w], writes=[Rgt], acc=True)
                        p1, Rp1 = PA.get()
                        mm_acc(p1[:, 0:w], Rp1, [(wa[:, k, j * 128:(j + 1) * 128], oaT[:, k, c0:c0 + w]) for k in range(8)], [Rwa, Rin5])
                        p2, Rp2 = PA.get()
                        mm_acc(p2[:, 0:w], Rp2, [(wb_[:, k, j * 128:(j + 1) * 128], obT5[:, k, c0:c0 + w]) for k in range(2)], [Rwb, Rin5])
                        m1, Rm1 = M1.get()
                        m2, Rm2 = M2.get()
                        op("dve", lambda e: e.tensor_tensor(out=m1[:], in0=p1[:, 0:w], in1=gtile[:, 0, :], op=ALU.mult), reads=[Rp1, Rgt], writes=[Rm1])
                        op("dve", lambda e: e.tensor_tensor(out=m2[:], in0=p2[:, 0:w], in1=gtile[:, 1, :], op=ALU.mult), reads=[Rp2, Rgt], writes=[Rm2])
                        op("pool", lambda e: e.tensor_tensor(out=mixed[:, c, c0:c0 + w], in0=m1[:], in1=m2[:], op=ALU.add), reads=[Rm1, Rm2], writes=[Rmx], acc=True)
            for grp in range(2):
                wo, Rwo = wload("w_out", l, grp * 512, 512, 8)
                for j in range(4):
                    c = grp * 4 + j
                    for (c0, w) in pieces(SEG):
                        xt_, Rxt = XT.get()
                        dma(xt_[:], Xin[c * 128:(c + 1) * 128, 2 + t0 + c0:2 + t0 + c0 + w], writes=[Rxt])
                        p1, Rp1 = PA.get()
                        mm_acc(p1[:, 0:w], Rp1, [(wo[:, k, j * 128:(j + 1) * 128], mixed[:, k, c0:c0 + w]) for k in range(8)], [Rwo, Rmx])
                        xo, Rxo = XO.get()
                        op("dve", lambda e: e.tensor_tensor(out=xo[:], in0=p1[:, 0:w], in1=xt_[:], op=ALU.add), reads=[Rp1, Rxt], writes=[Rxo])
                        dma(Xmid[c * 128:(c + 1) * 128, 2 + t0 + c0:2 + t0 + c0 + w], xo[:], reads=[Rxo], q="pool", acc=True)
        mk.pop()

        mk.push()
        FW = FB + 2
        xp = mk.sb([128, 8, 512], F32)
        sq = mk.sb([128, 8, 512], BF16)
        rs = mk.sb([128, 512], F32)
        Rt = Res()
        h2 = mk.sb([128, 8, FW], BF16)
        Rh2 = Res()
        RAW = Ring([(mk.sb([128, FW], F32), Res()) for _ in range(2)])
        ACC = Ring([(mk.sb([128, FB], F32), Res()) for _ in range(2)])
        UU = Ring([(mk.sb([128, FB], BF16), Res()) for _ in range(2)])
        aT = mk.sb([128, NFF, FB], BF16)
        RaT = Res()
        XT = Ring([(mk.sb([128, 512], F32), Res()) for _ in range(3)])
        XO = Ring([(mk.sb([128, 512], F32), Res()) for _ in range(3)])
        NBLK = SEG // FB
        for s in range(NSEG):
            for bk in range(NBLK):
                tb = s * SEG + bk * FB
                norm_h(h2, Rh2, Xmid, [RXmid], tb + 1, FW, g2, l, xp, sq, rs, Rt)
                fl_l = s if bk == 0 else None
                fl_r = s + 1 if bk == NBLK - 1 else None
                ups = [(0, 512), (512, 512), (1024, 512), (1536, 512), (2048, 512), (2560, 256)]
                for (col0, ncols) in ups:
                    wu, Rwu = wload("w_up", l, col0, ncols, 8)
                    wg, Rwg = wload("w_gate", l, col0, ncols, 8)
                    for j in range(ncols // 128):
                        f = col0 // 128 + j
                        raw, Rr = RAW.get()
                        for (c0, w) in pieces(FW):
                            p, Rp = PA.get()
                            mm_acc(p[:, 0:w], Rp, [(wu[:, kc, j * 128:(j + 1) * 128], h2[:, kc, c0:c0 + w]) for kc in range(8)], [Rwu, Rh2])
                            op("act", lambda e: e.copy(out=raw[:, c0:c0 + w], in_=p[:, 0:w]), reads=[Rp], writes=[Rr], acc=True)
                        if fl_l is not None:
                            op("dve", lambda e: e.tensor_scalar(out=raw[:, 0:1], in0=raw[:, 0:1], scalar1=flg[:, fl_l:fl_l + 1], scalar2=None, op0=ALU.mult), reads=[Rr, Rc], writes=[Rr])
                        if fl_r is not None:
                            op("dve", lambda e: e.tensor_scalar(out=raw[:, FB + 1:FB + 2], in0=raw[:, FB + 1:FB + 2], scalar1=flg[:, fl_r:fl_r + 1], scalar2=None, op0=ALU.mult), reads=[Rr, Rc], writes=[Rr])
                        ac, Rac = ACC.get()
                        op("dve", lambda e: e.tensor_scalar(out=ac[:], in0=raw[:, 0:FB], scalar1=cvF[:, l * 3, f:f + 1], scalar2=cvFb[:, l, f:f + 1], op0=ALU.mult, op1=ALU.add), reads=[Rr, Rc], writes=[Rac])
                        for k in range(1, 3):
                            op("dve", lambda e: e.scalar_tensor_tensor(out=ac[:], in0=raw[:, k:k + FB], scalar=cvF[:, l * 3 + k, f:f + 1], in1=ac[:], op0=ALU.mult, op1=ALU.add), reads=[Rr, Rc, Rac], writes=[Rac])
                        uu, Ruu = UU.get()
                        op("act", lambda e: e.activation(out=uu[:], in_=ac[:], func=AF.Silu), reads=[Rac], writes=[Ruu])
                        for (c0, w) in pieces(FB):
                            p, Rp = PA.get()
                            mm_acc(p[:, 0:w], Rp, [(wg[:, kc, j * 128:(j + 1) * 128], h2[:, kc, 1 + c0:1 + c0 + w]) for kc in range(8)], [Rwg, Rh2])
                            op("dve", lambda e: e.tensor_tensor(out=aT[:, f, c0:c0 + w], in0=p[:, 0:w], in1=uu[:, c0:c0 + w], op=ALU.mult), reads=[Rp, Ruu], writes=[RaT], acc=True)
                for c in range(8):
                    wd, Rwd = wload("w_down", l, c * 128, 128, NFF)
                    for (c0, w) in pieces(FB):
                        xt_, Rxt = XT.get()
                        dma(xt_[:], Xmid[c * 128:(c + 1) * 128, 2 + tb + c0:2 + tb + c0 + w], writes=[Rxt])
                        p, Rp = PA.get()
                        mm_acc(p[:, 0:w], Rp, [(wd[:, f, :], aT[:, f, c0:c0 + w]) for f in range(NFF)], [Rwd, RaT])
                        xo, Rxo = XO.get()
                        op("dve", lambda e: e.tensor_tensor(out=xo[:], in0=p[:, 0:w], in1=xt_[:], op=ALU.add), reads=[Rp, Rxt], writes=[Rxo])
                        if last:
                            dma(yT[c * 128:(c + 1) * 128, tb + c0:tb + c0 + w], xo[:], reads=[Rxo], q="pool")
                        else:
                            dma(XB[c * 128:(c + 1) * 128, 2 + tb + c0:2 + tb + c0 + w], xo[:], reads=[Rxo], q="pool", acc=True)
        mk.pop()

    mk.barrier()
    ninstr = mk.ninstr
    mk.close()
    return nc, ninstr


def _t5_bucket(delta):
    n = np.abs(delta)
    large = 8 + (np.log(np.maximum(n, 1).astype(np.float32) / np.float32(8)) / np.float32(np.log(1024 / 8)) * np.float32(8)).astype(np.int32)
    large = np.minimum(large, 15)
    return np.where(delta > 0, 16, 0) + np.where(n < 8, n, large)


def host_consts():
    i = np.arange(128)[:, None]
    j = np.arange(128)[None, :]
    f = lambda m: m.astype(np.float32)
    ident = f(i == j)
    U = f(i <= j)
    L = f(i >= j)
    pm_l_s = np.where(i > j, 0.0, BIG).astype(np.float32)
    pm_l_i = np.where(i >= j, 0.0, BIG).astype(np.float32)
    pm_u_s = np.where(j > i, 0.0, BIG).astype(np.float32)
    pm_u_i = np.where(j >= i, 0.0, BIG).astype(np.float32)
    ones = np.ones((128, 128), np.float32)
    bd = f((i // 64) == (j // 64)) / 64.0
    cst = np.concatenate([ident, U, L, pm_l_s, pm_l_i, pm_u_s, pm_u_i, ones, bd], axis=1)
    delta = np.arange(NF) - FOFF
    b = _t5_bucket(delta)
    oh = (b[None, :] == np.arange(32)[:, None]).astype(np.float32)
    valid = np.zeros((12, NF), np.float32)
    for g in range(3):
        d = DILS[g]
        v = ((delta % d) == 0) & (np.abs(delta) <= 64 * d)
        valid[4 * g:4 * g + 4] = v[None, :].astype(np.float32)
    return np.ascontiguousarray(cst), oh, valid


def core_inputs(x_tok, links, weights):
    NT = x_tok.shape[0]
    xTp = np.zeros((D, NT + 4), np.float32)
    xTp[:, 2:NT + 2] = x_tok.T
    cst, oh, valid = host_consts()
    m = {"xT": xTp, "flags": np.ascontiguousarray(np.broadcast_to(np.asarray(links, np.float32)[None, :], (128, len(links)))),
         "cst": cst, "oh": oh, "valid": valid}
    m.update(weights)
    return m


_CACHE = {}


def kernel(x_prompt, x_sample, ln1_g, w_in, conv_a, a_log, dt_bias, norm_a, qn_b, kn_b, rel_bias,
           w_oa, w_ob, w_out, ln2_g, w_up, w_gate, conv_ff, conv_ff_b, w_down):
    DEPTH = w_in.shape[0]
    NSEG, SEG = 5, 2048
    a32 = lambda a: np.ascontiguousarray(np.asarray(a, dtype=np.float32))
    weights = {"ln1_g": a32(ln1_g), "w_in": a32(w_in), "conv_a": a32(conv_a), "a_log": a32(a_log).reshape(DEPTH, 16),
               "dt_bias": a32(dt_bias).reshape(DEPTH, 16), "norm_a": a32(norm_a), "qn_b": a32(qn_b), "kn_b": a32(kn_b),
               "rel_bias": a32(rel_bias), "w_oa": a32(w_oa), "w_ob": a32(w_ob), "w_out": a32(w_out), "ln2_g": a32(ln2_g),
               "w_up": a32(w_up), "w_gate": a32(w_gate), "conv_ff": a32(conv_ff), "conv_ff_b": a32(conv_ff_b), "w_down": a32(w_down)}
    xp = np.asarray(x_prompt, np.float32)
    xs = np.asarray(x_sample, np.float32)
    in_maps = []
    own = []
    for c in range(8):
        if c < 2:
            toks = np.concatenate([xp[c], xs[c]], axis=0)
            links = [0, 1, 1, 1, 0, 0]
            own.append([("p", c), ("s", c)])
        else:
            ids = list(range(2 + 5 * (c - 2), 2 + 5 * (c - 1)))
            toks = np.concatenate([xs[i] for i in ids], axis=0)
            links = [0, 0, 0, 0, 0, 0]
            own.append([("s", i) for i in ids])
        in_maps.append(core_inputs(toks, links, weights))
    key = (NSEG, SEG, DEPTH)
    if key not in _CACHE:
        _CACHE[key] = build(NSEG, SEG, DEPTH)[0]
    nc = _CACHE[key]
    res = run_bass_kernel_spmd(nc, in_maps, core_ids=list(range(8)))
    y_prompt = np.empty_like(xp)
    y_sample = np.empty_like(xs)
    for c in range(8):
        y = res.results[c]["yT"].T
        pos = 0
        for kind, i in own[c]:
            n = xp.shape[1] if kind == "p" else xs.shape[1]
            (y_prompt if kind == "p" else y_sample)[i] = y[pos:pos + n]
            pos += n
    return (y_prompt, y_sample)
```
